# Optimizing a Trainium2 kernel written in Bass

```python
import math
import jax, jax.numpy as jnp
from jax import lax
import numpy as np

D_MODEL = 1024
BATCH = 8
SEQ = 4096
DEPTH = 4

HEAD_DIM = 64
ATT_GROUPS = ((128, 1), (512, 4), (2048, 16))
N_GROUPS = len(ATT_GROUPS)
HEADS_PER_GROUP = 4
N_ATT_QKV_HEADS = N_GROUPS * HEADS_PER_GROUP
D_ATT_QKV = N_ATT_QKV_HEADS * HEAD_DIM
D_ATT_OUT = HEADS_PER_GROUP * HEAD_DIM
ALIBI_MAX_BIAS = 8.0
D_LRU = (3 * D_MODEL) // 4
LRU_BLOCK = 64
N_LRU_BLOCKS = D_LRU // LRU_BLOCK
CONV_WIDTH = 4
LRU_C = 8.0
D_MIX = D_ATT_OUT + D_LRU
D_IN_PROJ = 3 * D_ATT_QKV + 2 * D_LRU
D_FF = 2816
FFN_RES_WEIGHT = 0.5
RMS_EPS = 1e-6
NEG_INF = -1e30

kernel_name = "hybrid_dilated_attn_rglru_macaron"


def rms_norm(x, gain):
    xf = x.astype(jnp.float32)
    y = xf * lax.rsqrt(jnp.mean(xf * xf, axis=-1, keepdims=True) + RMS_EPS)
    return (y * gain.astype(jnp.float32)).astype(x.dtype)


def swiglu(h, w_gate, w_up, w_down):
    return (jax.nn.silu(h @ w_gate) * (h @ w_up)) @ w_down


def alibi_slopes():
    h = jnp.arange(1, N_ATT_QKV_HEADS + 1, dtype=jnp.float32)
    return jnp.exp2(-ALIBI_MAX_BIAS * h / N_ATT_QKV_HEADS).reshape(N_GROUPS, HEADS_PER_GROUP)


def dilated_window_attention(q, k, v, window, dilation, slopes):
    B, S, H, Dh = q.shape
    n = window // dilation
    span = n * dilation
    s_pad = -(-S // span) * span
    L = s_pad // dilation
    nb = L // n

    def to_blocks(t):
        t = jnp.pad(t.astype(jnp.float32), ((0, 0), (0, s_pad - S), (0, 0), (0, 0)))
        t = t.reshape(B, L, dilation, H, Dh).transpose(0, 2, 3, 1, 4)
        return t.reshape(B, dilation, H, nb, n, Dh)

    def with_prev(t):
        prev = jnp.concatenate([jnp.zeros_like(t[:, :, :, :1]), t[:, :, :, :-1]], axis=3)
        return jnp.concatenate([prev, t], axis=4)

    qb = to_blocks(q)
    kw = with_prev(to_blocks(k))
    vw = with_prev(to_blocks(v))
    scores = jnp.einsum('brhnid,brhnkd->brhnik', qb, kw) / math.sqrt(Dh)
    qi = jnp.arange(n)[:, None]
    ki = jnp.arange(2 * n)[None, :]
    steps = n + qi - ki
    blk = jnp.arange(nb)[:, None, None]
    valid = (steps >= 0) & (steps <= n) & (blk * n + ki - n >= 0)
    bias = -slopes[:, None, None, None] * (dilation * steps).astype(jnp.float32)
    scores = jnp.where(valid, scores + bias, NEG_INF)
    lse = jax.nn.logsumexp(scores, axis=-1)
    p = jnp.exp(scores - lse[..., None])
    out = jnp.einsum('brhnik,brhnkd->brhnid', p, vw)
    out = out.reshape(B, dilation, H, L, Dh).transpose(0, 3, 1, 2, 4).reshape(B, s_pad, H, Dh)[:, :S]
    lse = lse.reshape(B, dilation, H, L).transpose(0, 3, 1, 2).reshape(B, s_pad, H)[:, :S]
    return out, lse


def causal_depthwise_conv(x, w, b):
    y = lax.conv_general_dilated(
        x, w[:, None, :].astype(x.dtype), window_strides=(1,),
        padding=((CONV_WIDTH - 1, 0),),
        dimension_numbers=('NWC', 'WIO', 'NWC'),
        feature_group_count=x.shape[-1])
    return y + b


def _linear_recurrence_combine(c1, c2):
    a1, b1 = c1
    a2, b2 = c2
    return a1 * a2, a2 * b1 + b2


def rg_lru(x, gate_a_w, gate_a_b, gate_x_w, gate_x_b, lam):
    B, S, C = x.shape
    xf = x.astype(jnp.float32)
    xh = xf.reshape(B, S, N_LRU_BLOCKS, LRU_BLOCK)
    r = jax.nn.sigmoid(jnp.einsum('bshi,hij->bshj', xh, gate_a_w.astype(jnp.float32)).reshape(B, S, C)
                       + gate_a_b.astype(jnp.float32))
    ig = jax.nn.sigmoid(jnp.einsum('bshi,hij->bshj', xh, gate_x_w.astype(jnp.float32)).reshape(B, S, C)
                        + gate_x_b.astype(jnp.float32))
    log_a = -LRU_C * r * jax.nn.softplus(-lam.astype(jnp.float32))
    a = jnp.exp(log_a)
    b = jnp.sqrt(-jnp.expm1(2.0 * log_a)) * (ig * xf)
    _, h = lax.associative_scan(_linear_recurrence_combine, (a, b), axis=1)
    return h.astype(x.dtype)


def setup_inputs(seed: int = 0) -> dict:
    key = jax.random.key(seed)
    ks = jax.random.split(key, 24)

    def normal(k, shape, scale):
        return jax.random.normal(k, shape, jnp.float32) * scale

    def gain(k, shape):
        return 1.0 + 0.02 * jax.random.normal(k, shape, jnp.float32)

    u = jax.random.uniform(ks[14], (DEPTH, D_LRU), jnp.float32, 0.9, 0.999)
    a0 = u ** (1.0 / LRU_C)
    lru_lambda = jnp.log(a0) - jnp.log1p(-a0)
    return {
        "x": normal(ks[0], (BATCH, SEQ, D_MODEL), 1.0),
        "ffn1_norm": gain(ks[1], (DEPTH, D_MODEL)),
        "ffn1_w_gate": normal(ks[2], (DEPTH, D_MODEL, D_FF), D_MODEL ** -0.5),
        "ffn1_w_up": normal(ks[3], (DEPTH, D_MODEL, D_FF), D_MODEL ** -0.5),
        "ffn1_w_down": normal(ks[4], (DEPTH, D_FF, D_MODEL), D_FF ** -0.5),
        "mix_norm": gain(ks[5], (DEPTH, D_MODEL)),
        "w_in": normal(ks[6], (DEPTH, D_MODEL, D_IN_PROJ), D_MODEL ** -0.5),
        "q_norm": gain(ks[7], (DEPTH, HEAD_DIM)),
        "k_norm": gain(ks[8], (DEPTH, HEAD_DIM)),
        "conv_w": normal(ks[9], (DEPTH, CONV_WIDTH, D_LRU), CONV_WIDTH ** -0.5),
        "conv_b": normal(ks[10], (DEPTH, D_LRU), 0.01),
        "gate_a_w": normal(ks[11], (DEPTH, N_LRU_BLOCKS, LRU_BLOCK, LRU_BLOCK), LRU_BLOCK ** -0.5),
        "gate_a_b": normal(ks[12], (DEPTH, D_LRU), 0.01),
        "gate_x_w": normal(ks[13], (DEPTH, N_LRU_BLOCKS, LRU_BLOCK, LRU_BLOCK), LRU_BLOCK ** -0.5),
        "gate_x_b": normal(ks[15], (DEPTH, D_LRU), 0.01),
        "lru_lambda": lru_lambda,
        "w_out": normal(ks[16], (DEPTH, D_MIX, D_MODEL), D_MIX ** -0.5),
        "ffn2_norm": gain(ks[17], (DEPTH, D_MODEL)),
        "ffn2_w_gate": normal(ks[18], (DEPTH, D_MODEL, D_FF), D_MODEL ** -0.5),
        "ffn2_w_up": normal(ks[19], (DEPTH, D_MODEL, D_FF), D_MODEL ** -0.5),
        "ffn2_w_down": normal(ks[20], (DEPTH, D_FF, D_MODEL), D_FF ** -0.5),
    }


def reference(x, ffn1_norm, ffn1_w_gate, ffn1_w_up, ffn1_w_down, mix_norm, w_in,
              q_norm, k_norm, conv_w, conv_b, gate_a_w, gate_a_b, gate_x_w, gate_x_b,
              lru_lambda, w_out, ffn2_norm, ffn2_w_gate, ffn2_w_up, ffn2_w_down):
    B, S, _ = x.shape
    slopes = alibi_slopes()
    for l in range(DEPTH):
        x = x + FFN_RES_WEIGHT * swiglu(rms_norm(x, ffn1_norm[l]), ffn1_w_gate[l], ffn1_w_up[l], ffn1_w_down[l])

        h = rms_norm(x, mix_norm[l])
        proj = h @ w_in[l]
        q, k, v, lru_x, lru_gate = jnp.split(
            proj, [D_ATT_QKV, 2 * D_ATT_QKV, 3 * D_ATT_QKV, 3 * D_ATT_QKV + D_LRU], axis=-1)
        q = rms_norm(q.reshape(B, S, N_GROUPS, HEADS_PER_GROUP, HEAD_DIM), q_norm[l])
        k = rms_norm(k.reshape(B, S, N_GROUPS, HEADS_PER_GROUP, HEAD_DIM), k_norm[l])
        v = v.reshape(B, S, N_GROUPS, HEADS_PER_GROUP, HEAD_DIM)

        outs, lses = [], []
        for g, (window, dilation) in enumerate(ATT_GROUPS):
            o_g, lse_g = dilated_window_attention(q[:, :, g], k[:, :, g], v[:, :, g], window, dilation, slopes[g])
            outs.append(o_g)
            lses.append(lse_g)
        w_grp = jax.nn.softmax(jnp.stack(lses, axis=0), axis=0)
        att = jnp.sum(w_grp[..., None] * jnp.stack(outs, axis=0), axis=0)
        att = att.reshape(B, S, D_ATT_OUT).astype(x.dtype)

        xc = causal_depthwise_conv(lru_x, conv_w[l], conv_b[l])
        y = rg_lru(xc, gate_a_w[l], gate_a_b[l], gate_x_w[l], gate_x_b[l], lru_lambda[l])
        y = y * jax.nn.gelu(lru_gate)

        x = x + jnp.concatenate([att, y], axis=-1) @ w_out[l]

        x = x + FFN_RES_WEIGHT * swiglu(rms_norm(x, ffn2_norm[l]), ffn2_w_gate[l], ffn2_w_up[l], ffn2_w_down[l])
    return x
```

```python
import math
from contextlib import ExitStack

import numpy as np
import concourse.bass as bass
import concourse.mybir as mybir
from concourse.bass_utils import run_bass_kernel_spmd

F32 = mybir.dt.float32
BF16 = mybir.dt.bfloat16
AF = mybir.ActivationFunctionType
ALU = mybir.AluOpType

D = 1024
S = 4096
DEPTH = 4
DFF = 2816
NF = DFF // 128
KC = D // 128
TT = 512
NT = S // TT
DLRU = 768
NLC = DLRU // 128
GROUPS = ((128, 1), (512, 4), (2048, 16))
EPS = 1e-6
NCORES = 8

P_G1 = 0
P_GM = P_G1 + DEPTH * KC
P_G2 = P_GM + DEPTH * KC
P_QN = P_G2 + DEPTH * KC
P_KN = P_QN + DEPTH
P_CW = P_KN + DEPTH
P_CB = P_CW + DEPTH * NLC * 4
P_GA = P_CB + DEPTH * NLC
P_GX = P_GA + DEPTH * NLC
P_LM = P_GX + DEPTH * NLC
NPRM = P_LM + DEPTH * NLC


class Eng:
    def __init__(self, name, h, sem):
        self.name = name
        self.h = h
        self.sem = sem
        self.cnt = 0
        self.pending = False
        self.known = {}


class Buf:
    __slots__ = ("name", "w", "wd", "r", "rd")

    def __init__(self, name):
        self.name = name
        self.w = {}
        self.wd = []
        self.r = {}
        self.rd = []


class Sched:
    KRING = 8

    def __init__(self, nc, stack):
        self.nc = nc

        def mk(n, h):
            return Eng(n, h, stack.enter_context(nc.semaphore("sem_" + n)))

        self.pe = mk("pe", nc.tensor)
        self.act = mk("act", nc.scalar)
        self.dve = mk("dve", nc.vector)
        self.pool = mk("pool", nc.gpsimd)
        self.sp = mk("sp", nc.sync)
        self.engs = {e.name: e for e in (self.pe, self.act, self.dve, self.pool, self.sp)}
        self.rings = {}
        for q in ("sp", "pool"):
            sems = [stack.enter_context(nc.semaphore(f"dma_{q}{i}")) for i in range(self.KRING)]
            self.rings[q] = {"sems": sems, "vals": [0] * self.KRING, "i": 0}
        self.n_wait = 0

    def _wait(self, E, tok):
        if tok[0] == "e":
            _, n, v = tok
            if E.known.get(n, 0) >= v:
                return
            E.h.wait_ge(self.engs[n].sem, v)
            E.known[n] = v
        else:
            _, sem, key, v = tok
            if E.known.get(key, 0) >= v:
                return
            E.h.wait_ge(sem, v)
            E.known[key] = v
        self.n_wait += 1

    def _deps(self, E, reads, writes):
        toks = []
        for b in reads:
            for n, v in b.w.items():
                if n == E.name and n == "pe":
                    continue
                toks.append(("e", n, v))
            toks.extend(b.wd)
        for b in writes:
            if b.r or b.rd:
                for n, v in b.r.items():
                    if n == E.name and n == "pe":
                        continue
                    toks.append(("e", n, v))
                toks.extend(b.rd)
        return toks

    def _commit(self, E, reads, writes, val=None, tok=None):
        for b in writes:
            if b.r or b.rd:
                b.w = {}
                b.wd = []
                b.r = {}
                b.rd = []
            if tok is None:
                b.w[E.name] = val
            else:
                b.wd.append(tok)
        for b in reads:
            if tok is None:
                b.r[E.name] = val
            else:
                b.rd.append(tok)

    def op(self, E, emit, reads=(), writes=(), inc=True):
        for t in self._deps(E, reads, writes):
            self._wait(E, t)
        ins = emit()
        if inc:
            E.cnt += 1
            ins.then_inc(E.sem, 1)
            val = E.cnt
            E.pending = False
        else:
            val = E.cnt + 1
            E.pending = True
        self._commit(E, reads, writes, val=val)
        return ins

    def dma(self, q, out, in_, reads=(), writes=()):
        E = self.engs[q]
        ring = self.rings[q]
        for t in self._deps(E, reads, writes):
            self._wait(E, t)
        slot = ring["i"] % self.KRING
        ring["i"] += 1
        sem = ring["sems"][slot]
        key = (q, slot)
        if ring["vals"][slot] > 0:
            self._wait(E, ("d", sem, key, ring["vals"][slot]))
        E.h.dma_start(out=out, in_=in_).then_inc(sem, 16)
        ring["vals"][slot] += 16
        tok = ("d", sem, key, ring["vals"][slot])
        self._commit(E, reads, writes, tok=tok)

    def barrier(self):
        assert not self.pe.pending
        for E in self.engs.values():
            for F in self.engs.values():
                if F.cnt > 0 and not (F is E and E.name in ("pe", "sp")):
                    self._wait(E, ("e", F.name, F.cnt))
            for q, ring in self.rings.items():
                for slot in range(self.KRING):
                    if ring["vals"][slot] > 0:
                        self._wait(E, ("d", ring["sems"][slot], (q, slot), ring["vals"][slot]))


class Rot:
    def __init__(self, items):
        self.items = items
        self.i = 0

    def get(self):
        it = self.items[self.i % len(self.items)]
        self.i += 1
        return it


def build_program(n_layers=DEPTH, stop_after=None):
    nc = bass.Bass("TRN2", target_bir_lowering=False)
    dt = nc.dram_tensor
    xT = dt("xT", [KC, 128, S], F32, kind="ExternalInput").ap()
    y = dt("y", [KC, 128, S], F32, kind="ExternalOutput").ap()
    wg_d = [dt(f"wg{i}", [DEPTH, NF, 128, KC, 128], F32, kind="ExternalInput").ap() for i in (1, 2)]
    wu_d = [dt(f"wu{i}", [DEPTH, NF, 128, KC, 128], F32, kind="ExternalInput").ap() for i in (1, 2)]
    wd_d = [dt(f"wd{i}", [DEPTH, NF, 128, D], F32, kind="ExternalInput").ap() for i in (1, 2)]
    win_d = dt("win", [DEPTH, 30, 128, KC, 128], F32, kind="ExternalInput").ap()
    wout_d = dt("wout", [DEPTH, KC, 128, D], F32, kind="ExternalInput").ap()
    bda_d = dt("bda", [DEPTH, NLC, 128, 128], F32, kind="ExternalInput").ap()
    bdx_d = dt("bdx", [DEPTH, NLC, 128, 128], F32, kind="ExternalInput").ap()
    prm_d = dt("prm", [128, NPRM], F32, kind="ExternalInput").ap()
    emask_d = dt("emask", [6, 128, 512], F32, kind="ExternalInput").ap()
    onesbd_d = dt("onesbd", [2, 128, 128], F32, kind="ExternalInput").ap()
    mixs = dt("mixs", [KC, 128, S], BF16, kind="Internal").ap()

    x_view_in = xT.rearrange("k p t -> p k t")
    y_view = y.rearrange("k p t -> p k t")
    mix_view = mixs.rearrange("k p t -> p k t")

    with ExitStack() as stack:
        sc = Sched(nc, stack)
        pe, act, dve, pool, sp = sc.pe, sc.act, sc.dve, sc.pool, sc.sp

        uid = [0]

        def sb(name, shape, dtype, st=None):
            uid[0] += 1
            return (st or stack).enter_context(nc.sbuf_tensor(f"{name}_{uid[0]}", shape, dtype))

        prm = sb("prm_t", [128, NPRM], F32)
        prm_b = Buf("prm")
        ones = sb("ones_t", [128, 2, 128], BF16)
        ones_b = Buf("ones")
        cs = sb("cs_t", [128, DEPTH * NLC], F32)
        cs_b = Buf("cs")
        cstmp = sb("cstmp_t", [128, DEPTH * NLC], F32)
        cstmp_b = Buf("cstmp")
        epsb = sb("epsb_t", [128, 1], F32)
        epsb_b = Buf("epsb")
        sc.op(dve, lambda: nc.vector.memset(epsb[:], EPS), writes=[epsb_b])
        ps_big = stack.enter_context(nc.psum_tensor("ps_big", [128, 4096], F32))
        ps_tiles = [ps_big[:, 512 * i:512 * (i + 1)] for i in range(8)]
        ps_bufs = [Buf(f"ps{i}") for i in range(8)]
        PS = list(zip(ps_tiles, ps_bufs))

        xd_b = [Buf(f"xd{t}") for t in range(NT)]
        mixd_b = [Buf(f"mixd{k}") for k in range(KC)]

        sc.dma("sp", prm[:], prm_d, writes=[prm_b])
        sc.dma("pool", ones[:, 0, :], onesbd_d[0], writes=[ones_b])
        sc.dma("pool", ones[:, 1, :], onesbd_d[1], writes=[ones_b])
        sc.op(act, lambda: nc.scalar.activation(out=cstmp[:], in_=prm[:, P_LM:P_LM + DEPTH * NLC], func=AF.Exp, scale=-1.0),
              reads=[prm_b], writes=[cstmp_b])
        sc.op(act, lambda: nc.scalar.activation(out=cstmp[:], in_=cstmp[:], func=AF.Ln, bias=1.0, scale=1.0),
              reads=[cstmp_b], writes=[cstmp_b])
        sc.op(dve, lambda: nc.vector.tensor_scalar(out=cs[:], in0=cstmp[:], scalar1=-8.0, scalar2=None, op0=ALU.mult),
              reads=[cstmp_b], writes=[cs_b])

        def mm(out, lhsT, rhs, start, stop, reads, writes, inc=None):
            return sc.op(pe, lambda: nc.tensor.matmul(out, lhsT, rhs, start=start, stop=stop),
                         reads=reads, writes=writes, inc=(stop if inc is None else inc))

        def rms_stats(x_t, x_b, sq_rot, psS, sd, sd_b, nfeat_inv, ones_idx, nk, lnexp=False):
            pst, psb = psS
            for k in range(nk):
                sq, sq_b = sq_rot.get()
                sc.op(act, lambda k=k, sq=sq: nc.scalar.activation(out=sq[:], in_=x_t[k], func=AF.Square),
                      reads=[x_b], writes=[sq_b])
                mm(pst[:], ones[:, ones_idx, :], sq[:], k == 0, k == nk - 1, [ones_b, sq_b], [psb], inc=True)
            if lnexp:
                sc.op(act, lambda: nc.scalar.activation(out=sd[:], in_=pst[:], func=AF.Ln, bias=epsb[:, 0:1], scale=nfeat_inv),
                      reads=[psb, epsb_b], writes=[sd_b])
                sc.op(act, lambda: nc.scalar.activation(out=sd[:], in_=sd[:], func=AF.Exp, scale=-0.5),
                      reads=[sd_b], writes=[sd_b])
                return
            sc.op(act, lambda: nc.scalar.activation(out=sd[:], in_=pst[:], func=AF.Sqrt, bias=EPS, scale=nfeat_inv),
                  reads=[psb], writes=[sd_b])
            sc.op(dve, lambda: nc.vector.reciprocal(out=sd[:], in_=sd[:]), reads=[sd_b], writes=[sd_b])

        def ffn_phase(l, which, src_view, first):
            gcol = (P_G1 if which == 0 else P_G2) + l * KC
            with ExitStack() as st:
                wgu = sb("wgu", [128, NF, 2, KC, 128], BF16, st)
                wdt = sb("wdt", [128, NF, D], BF16, st)
                wgu_b = [Buf(f"wgu{f}") for f in range(NF)]
                wd_b = [Buf(f"wd{f}") for f in range(NF)]
                xts = [sb(f"xt{i}", [128, KC, TT], F32, st) for i in range(2)]
                xt_b = [Buf(f"xt{i}") for i in range(2)]
                ht = sb("ht", [128, KC, TT], BF16, st)
                ht_b = Buf("ht")
                At = sb("At", [128, NF, TT], BF16, st)
                A_b = [Buf(f"A{f}") for f in range(NF)]
                sq_rot = Rot([(sb(f"sq{i}", [128, TT], BF16, st), Buf(f"sq{i}")) for i in range(3)])
                sg_rot = Rot([(sb(f"sg{i}", [128, TT], F32, st), Buf(f"sg{i}")) for i in range(2)])
                sd = sb("sd", [128, TT], F32, st)
                sd_b = Buf("sd")

                for f in range(NF):
                    sc.dma("pool", wgu[:, f, 0], wg_d[which][l, f], writes=[wgu_b[f]])
                    sc.dma("pool", wgu[:, f, 1], wu_d[which][l, f], writes=[wgu_b[f]])
                for f in range(NF):
                    sc.dma("pool", wdt[:, f, :], wd_d[which][l, f], writes=[wd_b[f]])

                G_rot = Rot([PS[0], PS[1]])
                U_rot = Rot([PS[2], PS[3]])
                Y_rot = Rot([PS[4], PS[5]])
                psS = PS[6]

                def load_x(tt):
                    i = tt % 2
                    sc.dma("sp", xts[i][:], src_view[:, :, tt * TT:(tt + 1) * TT],
                           reads=[] if first else [xd_b[tt]], writes=[xt_b[i]])

                def norm_stats(tt):
                    i = tt % 2
                    rms_stats([xts[i][:, k, :] for k in range(KC)], xt_b[i], sq_rot, psS, sd, sd_b, 1.0 / D, 0, KC)

                def norm_scale(tt):
                    i = tt % 2
                    for k in range(KC):
                        sc.op(dve, lambda k=k: nc.vector.scalar_tensor_tensor(
                            out=ht[:, k, :], in0=xts[i][:, k, :], scalar=prm[:, gcol + k:gcol + k + 1], in1=sd[:],
                            op0=ALU.mult, op1=ALU.mult), reads=[xt_b[i], sd_b, prm_b], writes=[ht_b])

                load_x(0)
                norm_stats(0)
                norm_scale(0)
                for tt in range(NT):
                    i = tt % 2
                    if tt + 1 < NT:
                        load_x(tt + 1)
                    for f in range(NF):
                        (pg, pgb), (pu, pub) = G_rot.get(), U_rot.get()
                        for k in range(KC):
                            mm(pg[:], wgu[:, f, 0, k, :], ht[:, k, :], k == 0, k == KC - 1, [wgu_b[f], ht_b], [pgb])
                        for k in range(KC):
                            mm(pu[:], wgu[:, f, 1, k, :], ht[:, k, :], k == 0, k == KC - 1, [wgu_b[f], ht_b], [pub])
                        sg, sg_b = sg_rot.get()
                        sc.op(act, lambda: nc.scalar.activation(out=sg[:], in_=pg[:], func=AF.Silu), reads=[pgb], writes=[sg_b])
                        sc.op(dve, lambda: nc.vector.tensor_tensor(out=At[:, f, :], in0=sg[:], in1=pu[:], op=ALU.mult),
                              reads=[sg_b, pub], writes=[A_b[f]])
                        if f == 12 and tt + 1 < NT:
                            norm_stats(tt + 1)
                    if tt + 1 < NT:
                        norm_scale(tt + 1)
                    for d in range(KC):
                        py, pyb = Y_rot.get()
                        for f in range(NF):
                            mm(py[:], wdt[:, f, d * 128:(d + 1) * 128], At[:, f, :], f == 0, f == NF - 1, [wd_b[f], A_b[f]], [pyb])
                        sc.op(dve, lambda d=d: nc.vector.scalar_tensor_tensor(
                            out=xts[i][:, d, :], in0=py[:], scalar=0.5, in1=xts[i][:, d, :], op0=ALU.mult, op1=ALU.add),
                            reads=[pyb, xt_b[i]], writes=[xt_b[i]])
                    sc.dma("sp", y_view[:, :, tt * TT:(tt + 1) * TT], xts[i][:], reads=[xt_b[i]], writes=[xd_b[tt]])
                sc.barrier()

        def mixer_phase(l, upto=None):
            with ExitStack() as st:
                hT = sb("hT", [128, KC, S], BF16, st)
                hT_b = [Buf(f"hT{t}") for t in range(NT)]
                hT_flat = hT[:].rearrange("p k t -> p (k t)")
                wrot = Rot([(sb(f"wp{i}", [128, KC, 128], BF16, st), Buf(f"wp{i}")) for i in range(6)])
                P_rot = Rot([PS[0], PS[1]])

                with ExitStack() as s1:
                    xts = [sb(f"mx{i}", [128, KC, TT], F32, s1) for i in range(2)]
                    xt_b = [Buf(f"mx{i}") for i in range(2)]
                    sq_rot = Rot([(sb(f"msq{i}", [128, TT], BF16, s1), Buf(f"msq{i}")) for i in range(3)])
                    sd_rot = Rot([(sb(f"msd{i}", [128, TT], F32, s1), Buf(f"msd{i}")) for i in range(2)])
                    gcol = P_GM + l * KC
                    sc.dma("sp", xts[0][:], y_view[:, :, 0:TT], reads=[xd_b[0]], writes=[xt_b[0]])
                    for tt in range(NT):
                        i = tt % 2
                        if tt + 1 < NT:
                            sc.dma("sp", xts[(tt + 1) % 2][:], y_view[:, :, (tt + 1) * TT:(tt + 2) * TT],
                                   reads=[xd_b[tt + 1]], writes=[xt_b[(tt + 1) % 2]])
                        sd, sd_b = sd_rot.get()
                        rms_stats([xts[i][:, k, :] for k in range(KC)], xt_b[i], sq_rot, PS[6 + (tt % 2)], sd, sd_b, 1.0 / D, 0, KC)
                        for k in range(KC):
                            sc.op(dve, lambda k=k: nc.vector.scalar_tensor_tensor(
                                out=hT[:, k, tt * TT:(tt + 1) * TT], in0=xts[i][:, k, :], scalar=prm[:, gcol + k:gcol + k + 1],
                                in1=sd[:], op0=ALU.mult, op1=ALU.mult), reads=[xt_b[i], sd_b, prm_b], writes=[hT_b[tt]])
                    sc.barrier()
                    if upto == "m1":
                        return

                def load_w(blk):
                    w, w_b = wrot.get()
                    sc.dma("pool", w[:], win_d[l, blk], writes=[w_b])
                    return w, w_b

                def proj_tile(w, w_b, tt, ps):
                    pst, psb = ps
                    for k in range(KC):
                        mm(pst[:], w[:, k, :], hT[:, k, tt * TT:(tt + 1) * TT], k == 0, k == KC - 1, [w_b, hT_b[tt]], [psb])

                with ExitStack() as s2:
                    NW = S + 4
                    LX = [sb(f"llx{i}", [128, NW], F32, s2) for i in range(2)]
                    LX_b = [Buf(f"llx{i}") for i in range(2)]
                    GL = [sb(f"lgl{i}", [128, S], BF16, s2) for i in range(2)]
                    GL_b = [Buf(f"lgl{i}") for i in range(2)]
                    XC = sb("lxc", [128, S], F32, s2)
                    XC_b = Buf("lxc")
                    IG = sb("lig", [128, S], F32, s2)
                    IG_b = Buf("lig")
                    PP = sb("lpp", [128, S], F32, s2)
                    PP_b = Buf("lpp")
                    xcb = sb("xcb", [128, S], BF16, s2)
                    xcb_b = Buf("xcb")
                    ybf = [sb(f"ybf{i}", [128, S], BF16, s2) for i in range(2)]
                    ybf_b = [Buf(f"ybf{i}") for i in range(2)]
                    bd_rot = Rot([(sb(f"bd{i}", [128, 128], BF16, s2), Buf(f"bd{i}")) for i in range(4)])
                    for i in range(2):
                        sc.op(dve, lambda i=i: nc.vector.memset(LX[i][:, 0:3], 0.0), writes=[LX_b[i]])

                    def projA_lx(c):
                        wx, wx_b = load_w(18 + c)
                        for tt in range(NT):
                            ps = P_rot.get()
                            proj_tile(wx, wx_b, tt, ps)
                            sc.op(act, lambda: nc.scalar.copy(out=LX[c % 2][:, 3 + tt * TT:3 + (tt + 1) * TT], in_=ps[0][:]),
                                  reads=[ps[1]], writes=[LX_b[c % 2]])

                    def projA_gl(c):
                        wgt, wgt_b = load_w(24 + c)
                        for tt in range(NT):
                            ps = P_rot.get()
                            proj_tile(wgt, wgt_b, tt, ps)
                            sc.op(act, lambda: nc.scalar.activation(out=GL[c % 2][:, tt * TT:(tt + 1) * TT], in_=ps[0][:],
                                                                    func=AF.Gelu_apprx_tanh), reads=[ps[1]], writes=[GL_b[c % 2]])

                    projA_lx(0)
                    projA_gl(0)
                    for c in range(NLC):
                        pc = l * NLC + c
                        lxt, lxb = LX[c % 2], LX_b[c % 2]
                        zd = lxt[:, 3:3 + S]
                        bda, bda_b = bd_rot.get()
                        bdx, bdx_b = bd_rot.get()
                        sc.dma("pool", bda[:], bda_d[l, c], writes=[bda_b])
                        sc.dma("pool", bdx[:], bdx_d[l, c], writes=[bdx_b])
                        cw0 = P_CW + pc * 4
                        sc.op(dve, lambda: nc.vector.tensor_scalar(
                            out=XC[:], in0=lxt[:, 0:S], scalar1=prm[:, cw0:cw0 + 1], scalar2=prm[:, P_CB + pc:P_CB + pc + 1],
                            op0=ALU.mult, op1=ALU.add), reads=[lxb, prm_b], writes=[XC_b])
                        for j in range(1, 4):
                            sc.op(dve, lambda j=j: nc.vector.scalar_tensor_tensor(
                                out=XC[:], in0=lxt[:, j:j + S], scalar=prm[:, cw0 + j:cw0 + j + 1], in1=XC[:],
                                op0=ALU.mult, op1=ALU.add), reads=[lxb, XC_b, prm_b], writes=[XC_b])
                        sc.op(pool, lambda: nc.gpsimd.tensor_copy(out=xcb[:], in_=XC[:]), reads=[XC_b], writes=[xcb_b])
                        for tt in range(NT):
                            ps = P_rot.get()
                            mm(ps[0][:], bda[:], xcb[:, tt * TT:(tt + 1) * TT], True, True, [bda_b, xcb_b], [ps[1]])
                            sc.op(act, lambda: nc.scalar.activation(out=zd[:, tt * TT:(tt + 1) * TT], in_=ps[0][:], func=AF.Sigmoid,
                                                                    bias=prm[:, P_GA + pc:P_GA + pc + 1]),
                                  reads=[ps[1], prm_b], writes=[lxb])
                        for tt in range(NT):
                            ps = P_rot.get()
                            mm(ps[0][:], bdx[:], xcb[:, tt * TT:(tt + 1) * TT], True, True, [bdx_b, xcb_b], [ps[1]])
                            sc.op(act, lambda: nc.scalar.activation(out=IG[:, tt * TT:(tt + 1) * TT], in_=ps[0][:], func=AF.Sigmoid,
                                                                    bias=prm[:, P_GX + pc:P_GX + pc + 1]),
                                  reads=[ps[1], prm_b], writes=[IG_b])
                        if c + 1 < NLC:
                            projA_lx(c + 1)
                        sc.op(dve, lambda: nc.vector.tensor_scalar(out=zd, in0=zd, scalar1=cs[:, pc:pc + 1], scalar2=None, op0=ALU.mult),
                              reads=[lxb, cs_b], writes=[lxb])
                        sc.op(dve, lambda: nc.vector.tensor_scalar(out=PP[:], in0=zd, scalar1=1.0 / 120.0, scalar2=1.0 / 24.0,
                                                                   op0=ALU.mult, op1=ALU.add), reads=[lxb], writes=[PP_b])
                        sc.op(dve, lambda: nc.vector.tensor_tensor(out=PP[:], in0=PP[:], in1=zd, op=ALU.mult), reads=[PP_b, lxb], writes=[PP_b])
                        for cc in (1.0 / 6.0, 0.5, 1.0):
                            sc.op(dve, lambda cc=cc: nc.vector.scalar_tensor_tensor(out=PP[:], in0=PP[:], scalar=cc, in1=zd, op0=ALU.add, op1=ALU.mult),
                                  reads=[PP_b, lxb], writes=[PP_b])
                        sc.op(dve, lambda: nc.vector.tensor_scalar_add(out=zd, in0=PP[:], scalar1=1.0), reads=[PP_b], writes=[lxb])
                        sc.op(dve, lambda: nc.vector.scalar_tensor_tensor(out=PP[:], in0=PP[:], scalar=2.0, in1=PP[:], op0=ALU.add, op1=ALU.mult),
                              reads=[PP_b], writes=[PP_b])
                        sc.op(act, lambda: nc.scalar.activation(out=PP[:], in_=PP[:], func=AF.Sqrt, scale=-1.0), reads=[PP_b], writes=[PP_b])
                        if c + 1 < NLC:
                            projA_gl(c + 1)
                        sc.op(dve, lambda: nc.vector.tensor_tensor(out=IG[:], in0=IG[:], in1=XC[:], op=ALU.mult),
                              reads=[IG_b, XC_b], writes=[IG_b])
                        sc.op(dve, lambda: nc.vector.tensor_tensor(out=IG[:], in0=IG[:], in1=PP[:], op=ALU.mult),
                              reads=[IG_b, PP_b], writes=[IG_b])
                        sc.op(dve, lambda: nc.vector.tensor_tensor_scan(out=XC[:], data0=zd, data1=IG[:], initial=0.0,
                                                                        op0=ALU.mult, op1=ALU.add),
                              reads=[lxb, IG_b, XC_b], writes=[XC_b])
                        yb, yb_b = ybf[c % 2], ybf_b[c % 2]
                        sc.op(dve, lambda: nc.vector.tensor_tensor(out=yb[:], in0=XC[:], in1=GL[c % 2][:], op=ALU.mult),
                              reads=[XC_b, GL_b[c % 2]], writes=[yb_b])
                        sc.dma("sp", mixs[2 + c], yb[:], reads=[yb_b], writes=[mixd_b[2 + c]])
                        if upto == "lru%d" % c:
                            sc.barrier()
                            return
                    sc.barrier()
                    if upto == "lru":
                        return

                with ExitStack() as s3:
                    acc = [sb(f"acc{i}", [128, S], F32, s3) for i in range(2)]
                    acc_b = [Buf(f"acc{i}") for i in range(2)]
                    qp = [sb(f"qp{s_}", [128, 32, 256], BF16, s3) for s_ in range(2)]
                    qp_b = [Buf(f"qp{s_}") for s_ in range(2)]
                    kTs = [sb(f"kT{s_}", [128, S], BF16, s3) for s_ in range(2)]
                    kT_bs = [Buf(f"kT{s_}") for s_ in range(2)]
                    va = [sb(f"va{s_}", [128, 32, 192], BF16, s3) for s_ in range(2)]
                    va_b = [Buf(f"va{s_}") for s_ in range(2)]
                    em_rot = Rot([(sb(f"em{i}", [128, 512], F32, s3), Buf(f"em{i}")) for i in range(2)])
                    ex_rot = Rot([(sb(f"ex{i}", [128, 512], F32, s3), Buf(f"ex{i}")) for i in range(2)])
                    pt_rot = Rot([(sb(f"pt{i}", [128, 512], BF16, s3), Buf(f"pt{i}")) for i in range(3)])
                    sq_rot = Rot([(sb(f"asq{i}", [128, TT], BF16, s3), Buf(f"asq{i}")) for i in range(2)])
                    sd_rot = Rot([(sb(f"asd{i}", [128, TT], F32, s3), Buf(f"asd{i}")) for i in range(2)])
                    tmp_rot = Rot([(sb(f"atm{i}", [128, TT], F32, s3), Buf(f"atm{i}")) for i in range(1)])
                    att_rot = Rot([(sb(f"att{i}", [128, TT], BF16, s3), Buf(f"att{i}")) for i in range(2)])
                    for s_ in range(2):
                        sc.op(dve, lambda s_=s_: nc.vector.memset(va[s_][:, :, 64:128], 1.0), writes=[va_b[s_]])
                        sc.op(dve, lambda s_=s_: nc.vector.memset(qp[s_][64:128, :, 0:128], 0.0), writes=[qp_b[s_]])
                        sc.op(dve, lambda s_=s_: nc.vector.memset(qp[s_][0:64, :, 128:256], 0.0), writes=[qp_b[s_]])
                    ST_rot = Rot([PS[4], PS[5], PS[6]])
                    A_rot = Rot([PS[0], PS[1], PS[2]])
                    PSTAT = PS[3]
                    po, po_b = PS[7]
                    em_of = {}

                    def proj_gen(hp, g, s_):
                        win_, r = GROUPS[g]
                        nb = (S // r) // 128
                        kT, kT_b = kTs[s_], kT_bs[s_]
                        vA, vA_b = va[s_], va_b[s_]
                        emt, emt_b = em_rot.get()
                        sc.dma("sp", emt[:], emask_d[g * 2 + hp], writes=[emt_b])
                        em_of[(hp, g)] = (emt, emt_b)
                        wq_ = load_w(2 * g + hp)
                        wk_ = load_w(6 + 2 * g + hp)
                        items = [(isq, tt) for isq in range(2) for tt in range(NT)]

                        def stepA(isq, tt):
                            w, w_b = (wq_, wk_)[isq]
                            ps = A_rot.get()
                            proj_tile(w, w_b, tt, ps)
                            return ps

                        def stepB(isq, tt, ps):
                            ncol = (P_QN + l, P_KN + l)[isq]
                            sd, sd_b = sd_rot.get()
                            rms_stats([ps[0][:]], ps[1], sq_rot, PSTAT, sd, sd_b, 1.0 / 64.0, 1, 1, lnexp=True)
                            if isq == 1:
                                if r == 1:
                                    o_ap = kT[:, tt * TT:(tt + 1) * TT]
                                    i_ap = ps[0][:]
                                    d_ap = sd[:]
                                else:
                                    na = TT // r
                                    o_ap = kT[:].rearrange("p (c l) -> p c l", c=r)[:, :, tt * na:(tt + 1) * na]
                                    i_ap = ps[0][:].rearrange("p (a c) -> p c a", c=r)
                                    d_ap = sd[:].rearrange("p (a c) -> p c a", c=r)
                                sc.op(dve, lambda: nc.vector.scalar_tensor_tensor(
                                    out=o_ap, in0=i_ap, scalar=prm[:, ncol:ncol + 1], in1=d_ap, op0=ALU.mult, op1=ALU.mult),
                                    reads=[ps[1], sd_b, prm_b], writes=[kT_b])
                            else:
                                for hh in range(2):
                                    rows = slice(hh * 64, hh * 64 + 64)
                                    hoff = hh * 128
                                    if r == 1:
                                        o_ap = qp[s_][rows, 4 * tt:4 * tt + 4, hoff:hoff + 128]
                                        i_ap = ps[0][rows, :].rearrange("p (j i) -> p j i", j=4)
                                        d_ap = sd[rows, :].rearrange("p (j i) -> p j i", j=4)
                                    elif r == 4:
                                        o_ap = qp[s_][rows, :, :].rearrange("p (c t) e -> p c t e", c=4)[:, :, tt, hoff:hoff + 128]
                                        i_ap = ps[0][rows, :].rearrange("p (a c) -> p c a", c=4)
                                        d_ap = sd[rows, :].rearrange("p (a c) -> p c a", c=4)
                                    else:
                                        i0 = hoff + 32 * (tt % 4)
                                        o_ap = qp[s_][rows, :, :].rearrange("p (c t) e -> p c t e", c=16)[:, :, tt // 4, i0:i0 + 32]
                                        i_ap = ps[0][rows, :].rearrange("p (a c) -> p c a", c=16)
                                        d_ap = sd[rows, :].rearrange("p (a c) -> p c a", c=16)
                                    sc.op(dve, lambda: nc.vector.scalar_tensor_tensor(
                                        out=o_ap, in0=i_ap, scalar=prm[rows, ncol:ncol + 1], in1=d_ap, op0=ALU.mult, op1=ALU.mult),
                                        reads=[ps[1], sd_b, prm_b], writes=[qp_b[s_]])

                        nxt_ps = stepA(*items[0])
                        for idx, it_ in enumerate(items):
                            cur_ps = nxt_ps
                            if idx + 1 < len(items):
                                nxt_ps = stepA(*items[idx + 1])
                            stepB(it_[0], it_[1], cur_ps)
                            yield
                        wv, wv_b = load_w(12 + 2 * g + hp)
                        for b4 in range(8):
                            pv, pvb = A_rot.get()
                            for j in range(4):
                                blk = b4 * 4 + j
                                c_, kb = divmod(blk, nb)
                                t0 = c_ + r * 128 * kb
                                tts = list(range(t0 // TT, (t0 + r * 127) // TT + 1))
                                for k in range(KC):
                                    base = k * S + t0
                                    mm(pv[:, j * 128:(j + 1) * 128], hT_flat[:, base:base + 127 * r + 1:r], wv[:, k, :], k == 0, k == KC - 1,
                                       [hT_b[t_] for t_ in tts] + [wv_b], [pvb])
                            pv3 = pv[:].rearrange("p (j e) -> p j e", j=4)
                            sc.op(act, lambda: nc.scalar.copy(out=vA[:, b4 * 4:b4 * 4 + 4, 0:64], in_=pv3[:, :, 0:64]),
                                  reads=[pvb], writes=[vA_b])
                            sc.op(act, lambda: nc.scalar.copy(out=vA[:, b4 * 4:b4 * 4 + 4, 128:192], in_=pv3[:, :, 64:128]),
                                  reads=[pvb], writes=[vA_b])
                            yield

                    LOOK = 2
                    PBURST = 12

                    def attn_gen(hp, g, s_):
                        win_, r = GROUPS[g]
                        nb = (S // r) // 128
                        kT, kT_b = kTs[s_], kT_bs[s_]
                        vA, vA_b = va[s_], va_b[s_]
                        emt, emt_b = em_of[(hp, g)]

                        def qk_block(blk):
                            c_, qb = divmod(blk, nb)
                            cur = slice(blk * 128, (blk + 1) * 128)
                            prv = slice((blk - 1) * 128, blk * 128)
                            stt, stb = ST_rot.get()
                            mm(stt[:, 0:256], kT[:, cur], qp[s_][:, blk, :], True, True, [kT_b, qp_b[s_]], [stb])
                            if qb > 0:
                                mm(stt[:, 256:512], kT[:, prv], qp[s_][:, blk, :], True, True, [kT_b, qp_b[s_]], [stb])
                            return stt, stb

                        pend = [qk_block(b_) for b_ in range(LOOK)]
                        for blk in range(32):
                            stt, stb = pend.pop(0)
                            if blk + LOOK < 32:
                                pend.append(qk_block(blk + LOOK))
                            c_, qb = divmod(blk, nb)
                            j = blk % 2
                            ex, ex_b = ex_rot.get()
                            sc.op(act, lambda: nc.scalar.activation(out=ex[:], in_=stt[:], func=AF.Exp, scale=0.125),
                                  reads=[stb], writes=[ex_b])
                            pt, pt_b = pt_rot.get()
                            sc.op(dve, lambda: nc.vector.tensor_tensor(out=pt[:], in0=ex[:], in1=emt[:], op=ALU.mult),
                                  reads=[ex_b, emt_b], writes=[pt_b])
                            for hh in range(2):
                                vc = slice(hh * 64, hh * 64 + 128)
                                oc = slice(hh * 256 + j * 128, hh * 256 + (j + 1) * 128)
                                mm(po[:, oc], vA[:, blk, vc], pt[:, hh * 128:hh * 128 + 128], True, qb == 0, [vA_b, pt_b], [po_b])
                                if qb > 0:
                                    mm(po[:, oc], vA[:, blk - 1, vc], pt[:, 256 + hh * 128:256 + hh * 128 + 128], False, True, [vA_b, pt_b], [po_b])
                            if j == 1:
                                blk0 = blk - 1
                                c0, qb0 = divmod(blk0, nb)
                                start = c0 + r * 128 * qb0
                                for hh in range(2):
                                    a_ap = acc[hh][:, start:start + 255 * r + 1:r]
                                    p_ap = po[:, hh * 256:(hh + 1) * 256]
                                    if g == 0:
                                        sc.op(dve, lambda: nc.vector.tensor_copy(out=a_ap, in_=p_ap), reads=[po_b], writes=[acc_b[hh]])
                                    else:
                                        sc.op(dve, lambda: nc.vector.tensor_tensor(out=a_ap, in0=a_ap, in1=p_ap, op=ALU.add),
                                              reads=[po_b, acc_b[hh]], writes=[acc_b[hh]])
                            yield

                    def normalize(hp):
                        for tt in range(NT):
                            cols = slice(tt * TT, (tt + 1) * TT)
                            tm, tm_b = tmp_rot.get()
                            at, at_b = att_rot.get()
                            sc.op(dve, lambda: nc.vector.tensor_copy(out=tm[0:64, :], in_=acc[0][64:128, cols]), reads=[acc_b[0]], writes=[tm_b])
                            sc.op(dve, lambda: nc.vector.tensor_copy(out=tm[64:128, :], in_=acc[1][0:64, cols]), reads=[acc_b[1]], writes=[tm_b])
                            sc.op(dve, lambda: nc.vector.reciprocal(out=tm[:], in_=tm[:]), reads=[tm_b], writes=[tm_b])
                            sc.op(dve, lambda: nc.vector.tensor_tensor(out=at[0:64, :], in0=acc[0][0:64, cols], in1=tm[0:64, :], op=ALU.mult),
                                  reads=[acc_b[0], tm_b], writes=[at_b])
                            sc.op(dve, lambda: nc.vector.tensor_tensor(out=at[64:128, :], in0=acc[1][64:128, cols], in1=tm[64:128, :], op=ALU.mult),
                                  reads=[acc_b[1], tm_b], writes=[at_b])
                            sc.dma("sp", mixs[hp][:, cols], at[:], reads=[at_b], writes=[mixd_b[hp]])

                    seq = [(hp, g) for hp in range(2) for g in range(3)]
                    for _ in proj_gen(seq[0][0], seq[0][1], 0):
                        pass
                    for i_, (hp, g) in enumerate(seq):
                        A = attn_gen(hp, g, i_ % 2)
                        P = proj_gen(seq[i_ + 1][0], seq[i_ + 1][1], (i_ + 1) % 2) if i_ + 1 < len(seq) else None
                        n_p = 24
                        done_p = 0
                        for step, _ in enumerate(A):
                            if P is not None:
                                want = ((((step + 1) * n_p) // 32) // PBURST) * PBURST
                                while done_p < want:
                                    next(P, None)
                                    done_p += 1
                        if P is not None:
                            for _ in P:
                                pass
                        if g == 2:
                            normalize(hp)
                    sc.barrier()
                    if upto == "attn":
                        return
            with ExitStack() as s4:
                wo = sb("wo", [128, KC, D], BF16, s4)
                wo_b = [Buf(f"wo{k}") for k in range(KC)]
                xts = [sb(f"ox{i}", [128, KC, TT], F32, s4) for i in range(2)]
                xt_b = [Buf(f"ox{i}") for i in range(2)]
                mts = [sb(f"om{i}", [128, KC, TT], BF16, s4) for i in range(2)]
                mt_b = [Buf(f"om{i}") for i in range(2)]
                Y_rot = Rot([PS[0], PS[1], PS[2], PS[3]])
                for k in range(KC):
                    sc.dma("pool", wo[:, k, :], wout_d[l, k], writes=[wo_b[k]])

                def load(tt):
                    i = tt % 2
                    sc.dma("sp", xts[i][:], y_view[:, :, tt * TT:(tt + 1) * TT], reads=[xd_b[tt]], writes=[xt_b[i]])
                    sc.dma("sp", mts[i][:], mix_view[:, :, tt * TT:(tt + 1) * TT], reads=mixd_b, writes=[mt_b[i]])

                load(0)
                for tt in range(NT):
                    i = tt % 2
                    if tt + 1 < NT:
                        load(tt + 1)
                    for d in range(KC):
                        py, pyb = Y_rot.get()
                        for k in range(KC):
                            mm(py[:], wo[:, k, d * 128:(d + 1) * 128], mts[i][:, k, :], k == 0, k == KC - 1, [wo_b[k], mt_b[i]], [pyb])
                        sc.op(dve, lambda d=d: nc.vector.tensor_tensor(out=xts[i][:, d, :], in0=py[:], in1=xts[i][:, d, :], op=ALU.add),
                              reads=[pyb, xt_b[i]], writes=[xt_b[i]])
                    sc.dma("sp", y_view[:, :, tt * TT:(tt + 1) * TT], xts[i][:], reads=[xt_b[i]], writes=[xd_b[tt]])
                sc.barrier()

        done = False
        for l in range(n_layers):
            ffn_phase(l, 0, x_view_in if l == 0 else y_view, first=(l == 0))
            if stop_after == (l, "ffn1"):
                done = True
                break
            if stop_after is not None and stop_after[0] == l and len(stop_after) == 3:
                mixer_phase(l, stop_after[2])
                break
            mixer_phase(l)
            if stop_after == (l, "mixer"):
                done = True
                break
            ffn_phase(l, 1, y_view, first=False)
        sc.barrier()
        build_program.stats = dict(cnt={n: e.cnt for n, e in sc.engs.items()}, waits=sc.n_wait)
    return nc


def _slopes():
    h = np.arange(1, 13, dtype=np.float64)
    return np.exp2(-8.0 * h / 12.0).reshape(3, 4)


def _emask():
    sl = _slopes()
    i = np.arange(128)[:, None].astype(np.float64)
    j = np.arange(128)[None, :].astype(np.float64)
    out = np.zeros((6, 128, 512), np.float32)
    for g, (_, r) in enumerate(GROUPS):
        for hp in range(2):
            for hh in range(2):
                a = sl[g, 2 * hp + hh] * r
                cur = np.where(j >= i, np.exp(-a * (j - i)), 0.0)
                prv = np.where(i >= j, np.exp(-a * (128.0 + j - i)), 0.0)
                out[g * 2 + hp, :, hh * 128:hh * 128 + 128] = cur
                out[g * 2 + hp, :, 256 + hh * 128:256 + hh * 128 + 128] = prv
    return out


def _prep_weights(inp):
    f32 = np.float32
    c = np.ascontiguousarray
    out = {}
    for i, nm in ((1, "ffn1"), (2, "ffn2")):
        for key, src in ((f"wg{i}", f"{nm}_w_gate"), (f"wu{i}", f"{nm}_w_up")):
            w = np.asarray(inp[src], f32).reshape(DEPTH, KC, 128, NF, 128)
            out[key] = c(w.transpose(0, 3, 2, 1, 4))
        out[f"wd{i}"] = c(np.asarray(inp[f"{nm}_w_down"], f32).reshape(DEPTH, NF, 128, D))
    w = np.asarray(inp["w_in"], f32).reshape(DEPTH, KC, 128, 30, 128)
    out["win"] = c(w.transpose(0, 3, 2, 1, 4))
    out["wout"] = c(np.asarray(inp["w_out"], f32).reshape(DEPTH, KC, 128, D))
    for key, src in (("bda", "gate_a_w"), ("bdx", "gate_x_w")):
        gw = np.asarray(inp[src], f32)
        bd = np.zeros((DEPTH, NLC, 128, 128), f32)
        for cidx in range(NLC):
            bd[:, cidx, 0:64, 0:64] = gw[:, 2 * cidx]
            bd[:, cidx, 64:128, 64:128] = gw[:, 2 * cidx + 1]
        out[key] = bd
    prm = np.zeros((128, NPRM), f32)
    for base, src in ((P_G1, "ffn1_norm"), (P_GM, "mix_norm"), (P_G2, "ffn2_norm")):
        g_ = np.asarray(inp[src], f32).reshape(DEPTH, KC, 128)
        prm[:, base:base + DEPTH * KC] = g_.transpose(2, 0, 1).reshape(128, DEPTH * KC)
    prm[:, P_QN:P_QN + DEPTH] = np.tile(np.asarray(inp["q_norm"], f32).T, (2, 1))
    prm[:, P_KN:P_KN + DEPTH] = np.tile(np.asarray(inp["k_norm"], f32).T, (2, 1))
    cw = np.asarray(inp["conv_w"], f32).reshape(DEPTH, 4, NLC, 128)
    prm[:, P_CW:P_CW + DEPTH * NLC * 4] = cw.transpose(3, 0, 2, 1).reshape(128, DEPTH * NLC * 4)
    for base, src in ((P_CB, "conv_b"), (P_GA, "gate_a_b"), (P_GX, "gate_x_b"), (P_LM, "lru_lambda")):
        v = np.asarray(inp[src], f32).reshape(DEPTH, NLC, 128)
        prm[:, base:base + DEPTH * NLC] = v.transpose(2, 0, 1).reshape(128, DEPTH * NLC)
    out["prm"] = prm
    out["emask"] = _emask()
    ob = np.zeros((2, 128, 128), f32)
    ob[0] = 1.0
    ob[1, 0:64, 0:64] = 1.0
    ob[1, 64:128, 64:128] = 1.0
    out["onesbd"] = ob
    return out


def _run(inp, cores, n_layers=DEPTH, stop_after=None, trace=False):
    shared = _prep_weights(inp)
    x = np.asarray(inp["x"], np.float32)
    nc = build_program(n_layers, stop_after)
    in_maps = []
    for b in cores:
        m = dict(shared)
        m["xT"] = np.ascontiguousarray(x[b].T).reshape(KC, 128, S)
        in_maps.append(m)
    res = run_bass_kernel_spmd(nc, in_maps, core_ids=list(range(len(cores))), trace=trace)
    outs = [np.ascontiguousarray(r["y"].reshape(D, S).T) for r in res.results]
    return outs, res


def kernel(**inputs):
    outs, _ = _run(inputs, list(range(NCORES)))
    return np.stack(outs, axis=0).astype(np.float32)
```

```python
import math
from contextlib import ExitStack

import numpy as np
import concourse.bass as bass
import concourse.mybir as mybir
from concourse.bass_utils import run_bass_kernel_spmd

F32 = mybir.dt.float32
BF16 = mybir.dt.bfloat16
AF = mybir.ActivationFunctionType
ALU = mybir.AluOpType

D = 1024
S = 4096
DEPTH = 4
DFF = 2816
NF = DFF // 128
KC = D // 128
TT = 512
NT = S // TT
DLRU = 768
NLC = DLRU // 128
GROUPS = ((128, 1), (512, 4), (2048, 16))
EPS = 1e-6
NCORES = 8
PREFETCH_FFN2 = False

P_G1 = 0
P_GM = P_G1 + DEPTH * KC
P_G2 = P_GM + DEPTH * KC
P_QN = P_G2 + DEPTH * KC
P_KN = P_QN + DEPTH
P_CW = P_KN + DEPTH
P_CB = P_CW + DEPTH * NLC * 4
P_GA = P_CB + DEPTH * NLC
P_GX = P_GA + DEPTH * NLC
P_LM = P_GX + DEPTH * NLC
NPRM = P_LM + DEPTH * NLC


class Eng:
    def __init__(self, name, h, sem):
        self.name = name
        self.h = h
        self.sem = sem
        self.cnt = 0
        self.pending = False
        self.known = {}


class Buf:
    __slots__ = ("name", "w", "wd", "r", "rd")

    def __init__(self, name):
        self.name = name
        self.w = {}
        self.wd = []
        self.r = {}
        self.rd = []


class Sched:
    KRING = 8

    def __init__(self, nc, stack):
        self.nc = nc

        def mk(n, h):
            return Eng(n, h, stack.enter_context(nc.semaphore("sem_" + n)))

        self.pe = mk("pe", nc.tensor)
        self.act = mk("act", nc.scalar)
        self.dve = mk("dve", nc.vector)
        self.pool = mk("pool", nc.gpsimd)
        self.sp = mk("sp", nc.sync)
        self.engs = {e.name: e for e in (self.pe, self.act, self.dve, self.pool, self.sp)}
        self.rings = {}
        for q in ("sp", "pool"):
            sems = [stack.enter_context(nc.semaphore(f"dma_{q}{i}")) for i in range(self.KRING)]
            self.rings[q] = {"sems": sems, "vals": [0] * self.KRING, "i": 0}
        self.n_wait = 0

    def _wait(self, E, tok):
        if tok[0] == "e":
            _, n, v = tok
            if E.known.get(n, 0) >= v:
                return
            E.h.wait_ge(self.engs[n].sem, v)
            E.known[n] = v
        else:
            _, sem, key, v = tok
            if E.known.get(key, 0) >= v:
                return
            E.h.wait_ge(sem, v)
            E.known[key] = v
        self.n_wait += 1

    def _deps(self, E, reads, writes):
        toks = []
        for b in reads:
            for n, v in b.w.items():
                if n == E.name and n == "pe":
                    continue
                toks.append(("e", n, v))
            toks.extend(b.wd)
        for b in writes:
            if b.r or b.rd:
                for n, v in b.r.items():
                    if n == E.name and n == "pe":
                        continue
                    toks.append(("e", n, v))
                toks.extend(b.rd)
        return toks

    def _commit(self, E, reads, writes, val=None, tok=None):
        for b in writes:
            if b.r or b.rd:
                b.w = {}
                b.wd = []
                b.r = {}
                b.rd = []
            if tok is None:
                b.w[E.name] = val
            else:
                b.wd.append(tok)
        for b in reads:
            if tok is None:
                b.r[E.name] = val
            else:
                b.rd.append(tok)

    def op(self, E, emit, reads=(), writes=(), inc=True):
        for t in self._deps(E, reads, writes):
            self._wait(E, t)
        ins = emit()
        if inc:
            E.cnt += 1
            ins.then_inc(E.sem, 1)
            val = E.cnt
            E.pending = False
        else:
            val = E.cnt + 1
            E.pending = True
        self._commit(E, reads, writes, val=val)
        return ins

    def dma(self, q, out, in_, reads=(), writes=()):
        E = self.engs[q]
        ring = self.rings[q]
        for t in self._deps(E, reads, writes):
            self._wait(E, t)
        slot = ring["i"] % self.KRING
        ring["i"] += 1
        sem = ring["sems"][slot]
        key = (q, slot)
        if ring["vals"][slot] > 0:
            self._wait(E, ("d", sem, key, ring["vals"][slot]))
        E.h.dma_start(out=out, in_=in_).then_inc(sem, 16)
        ring["vals"][slot] += 16
        tok = ("d", sem, key, ring["vals"][slot])
        self._commit(E, reads, writes, tok=tok)

    def barrier(self):
        assert not self.pe.pending
        for E in self.engs.values():
            for F in self.engs.values():
                if F.cnt > 0 and not (F is E and E.name in ("pe", "sp")):
                    self._wait(E, ("e", F.name, F.cnt))
            for q, ring in self.rings.items():
                for slot in range(self.KRING):
                    if ring["vals"][slot] > 0:
                        self._wait(E, ("d", ring["sems"][slot], (q, slot), ring["vals"][slot]))


class Rot:
    def __init__(self, items):
        self.items = items
        self.i = 0

    def get(self):
        it = self.items[self.i % len(self.items)]
        self.i += 1
        return it


def build_program(n_layers=DEPTH, stop_after=None):
    nc = bass.Bass("TRN2", target_bir_lowering=False)
    dt = nc.dram_tensor
    xT = dt("xT", [KC, 128, S], F32, kind="ExternalInput").ap()
    y = dt("y", [KC, 128, S], F32, kind="ExternalOutput").ap()
    wg_d = [dt(f"wg{i}", [DEPTH, NF, 128, KC, 128], F32, kind="ExternalInput").ap() for i in (1, 2)]
    wu_d = [dt(f"wu{i}", [DEPTH, NF, 128, KC, 128], F32, kind="ExternalInput").ap() for i in (1, 2)]
    wd_d = [dt(f"wd{i}", [DEPTH, NF, 128, D], F32, kind="ExternalInput").ap() for i in (1, 2)]
    win_d = dt("win", [DEPTH, 30, 128, KC, 128], F32, kind="ExternalInput").ap()
    wout_d = dt("wout", [DEPTH, KC, 128, D], F32, kind="ExternalInput").ap()
    bda_d = dt("bda", [DEPTH, NLC, 128, 128], F32, kind="ExternalInput").ap()
    bdx_d = dt("bdx", [DEPTH, NLC, 128, 128], F32, kind="ExternalInput").ap()
    prm_d = dt("prm", [128, NPRM], F32, kind="ExternalInput").ap()
    emask_d = dt("emask", [6, 128, 512], F32, kind="ExternalInput").ap()
    onesbd_d = dt("onesbd", [2, 128, 128], F32, kind="ExternalInput").ap()
    mixs = dt("mixs", [KC, 128, S], BF16, kind="Internal").ap()

    x_view_in = xT.rearrange("k p t -> p k t")
    y_view = y.rearrange("k p t -> p k t")
    mix_view = mixs.rearrange("k p t -> p k t")

    with ExitStack() as stack:
        sc = Sched(nc, stack)
        pe, act, dve, pool, sp = sc.pe, sc.act, sc.dve, sc.pool, sc.sp

        uid = [0]

        def sb(name, shape, dtype, st=None):
            uid[0] += 1
            return (st or stack).enter_context(nc.sbuf_tensor(f"{name}_{uid[0]}", shape, dtype))

        prm = sb("prm_t", [128, NPRM], F32)
        prm_b = Buf("prm")
        ones = sb("ones_t", [128, 2, 128], BF16)
        ones_b = Buf("ones")
        cs = sb("cs_t", [128, DEPTH * NLC], F32)
        cs_b = Buf("cs")
        cstmp = sb("cstmp_t", [128, DEPTH * NLC], F32)
        cstmp_b = Buf("cstmp")
        epsb = sb("epsb_t", [128, 1], F32)
        epsb_b = Buf("epsb")
        sc.op(dve, lambda: nc.vector.memset(epsb[:], EPS), writes=[epsb_b])
        ps_big = stack.enter_context(nc.psum_tensor("ps_big", [128, 4096], F32))
        ps_tiles = [ps_big[:, 512 * i:512 * (i + 1)] for i in range(8)]
        ps_bufs = [Buf(f"ps{i}") for i in range(8)]
        PS = list(zip(ps_tiles, ps_bufs))

        xd_b = [Buf(f"xd{t}") for t in range(NT)]
        mixd_b = [Buf(f"mixd{k}") for k in range(KC)]

        sc.dma("sp", prm[:], prm_d, writes=[prm_b])
        sc.dma("pool", ones[:, 0, :], onesbd_d[0], writes=[ones_b])
        sc.dma("pool", ones[:, 1, :], onesbd_d[1], writes=[ones_b])
        sc.op(act, lambda: nc.scalar.activation(out=cstmp[:], in_=prm[:, P_LM:P_LM + DEPTH * NLC], func=AF.Exp, scale=-1.0),
              reads=[prm_b], writes=[cstmp_b])
        sc.op(act, lambda: nc.scalar.activation(out=cstmp[:], in_=cstmp[:], func=AF.Ln, bias=1.0, scale=1.0),
              reads=[cstmp_b], writes=[cstmp_b])
        sc.op(dve, lambda: nc.vector.tensor_scalar(out=cs[:], in0=cstmp[:], scalar1=-8.0, scalar2=None, op0=ALU.mult),
              reads=[cstmp_b], writes=[cs_b])

        ckt = sb("ck_t", [128, 5, DEPTH * NLC], F32)
        ck_b = Buf("ck")
        sc.op(dve, lambda: nc.vector.tensor_copy(out=ckt[:, 0, :], in_=cs[:]), reads=[cs_b], writes=[ck_b])
        for k_ in range(1, 5):
            sc.op(dve, lambda k_=k_: nc.vector.scalar_tensor_tensor(out=ckt[:, k_, :], in0=ckt[:, k_ - 1, :], scalar=1.0 / (k_ + 1), in1=cs[:],
                                                                    op0=ALU.mult, op1=ALU.mult), reads=[ck_b, cs_b], writes=[ck_b])

        def mm(out, lhsT, rhs, start, stop, reads, writes, inc=None):
            return sc.op(pe, lambda: nc.tensor.matmul(out, lhsT, rhs, start=start, stop=stop),
                         reads=reads, writes=writes, inc=(stop if inc is None else inc))

        def rms_stats(x_t, x_b, sq_rot, psS, sd, sd_b, nfeat_inv, ones_idx, nk, lnexp=False):
            pst, psb = psS
            for k in range(nk):
                sq, sq_b = sq_rot.get()
                sc.op(act, lambda k=k, sq=sq: nc.scalar.activation(out=sq[:], in_=x_t[k], func=AF.Square),
                      reads=[x_b], writes=[sq_b])
                mm(pst[:], ones[:, ones_idx, :], sq[:], k == 0, k == nk - 1, [ones_b, sq_b], [psb], inc=True)
            if lnexp:
                sc.op(act, lambda: nc.scalar.activation(out=sd[:], in_=pst[:], func=AF.Ln, bias=epsb[:, 0:1], scale=nfeat_inv),
                      reads=[psb, epsb_b], writes=[sd_b])
                sc.op(act, lambda: nc.scalar.activation(out=sd[:], in_=sd[:], func=AF.Exp, scale=-0.5),
                      reads=[sd_b], writes=[sd_b])
                return
            sc.op(act, lambda: nc.scalar.activation(out=sd[:], in_=pst[:], func=AF.Sqrt, bias=EPS, scale=nfeat_inv),
                  reads=[psb], writes=[sd_b])
            sc.op(dve, lambda: nc.vector.reciprocal(out=sd[:], in_=sd[:]), reads=[sd_b], writes=[sd_b])

        def ffn_alloc(st):
            return dict(wgu=sb("wgu", [128, NF, 2, KC, 128], BF16, st), wdt=sb("wdt", [128, NF, D], BF16, st),
                        wgu_b=[Buf(f"wgu{f}") for f in range(NF)], wd_b=[Buf(f"wd{f}") for f in range(NF)])

        def ffn_load(W, l, which):
            for f in range(NF):
                sc.dma("pool", W["wgu"][:, f, 0], wg_d[which][l, f], writes=[W["wgu_b"][f]])
                sc.dma("pool", W["wgu"][:, f, 1], wu_d[which][l, f], writes=[W["wgu_b"][f]])
            for f in range(NF):
                sc.dma("pool", W["wdt"][:, f, :], wd_d[which][l, f], writes=[W["wd_b"][f]])

        def ffn_phase(l, which, src_view, first, W=None):
            gcol = (P_G1 if which == 0 else P_G2) + l * KC
            with ExitStack() as st:
                preloaded = W is not None
                if W is None:
                    W = ffn_alloc(st)
                wgu, wdt, wgu_b, wd_b = W["wgu"], W["wdt"], W["wgu_b"], W["wd_b"]
                xts = [sb(f"xt{i}", [128, KC, TT], F32, st) for i in range(2)]
                xt_b = [Buf(f"xt{i}") for i in range(2)]
                ht = sb("ht", [128, KC, TT], BF16, st)
                ht_b = Buf("ht")
                At = sb("At", [128, NF, TT], BF16, st)
                A_b = [Buf(f"A{f}") for f in range(NF)]
                sq_rot = Rot([(sb(f"sq{i}", [128, TT], BF16, st), Buf(f"sq{i}")) for i in range(3)])
                sg_rot = Rot([(sb(f"sg{i}", [128, TT], F32, st), Buf(f"sg{i}")) for i in range(2)])
                sd = sb("sd", [128, TT], F32, st)
                sd_b = Buf("sd")

                if not preloaded:
                    ffn_load(W, l, which)

                G_rot = Rot([PS[0], PS[1]])
                U_rot = Rot([PS[2], PS[3]])
                Y_rot = Rot([PS[4], PS[5]])
                psS = PS[6]

                def load_x(tt):
                    i = tt % 2
                    sc.dma("sp", xts[i][:], src_view[:, :, tt * TT:(tt + 1) * TT],
                           reads=[] if first else [xd_b[tt]], writes=[xt_b[i]])

                def norm_stats(tt):
                    i = tt % 2
                    rms_stats([xts[i][:, k, :] for k in range(KC)], xt_b[i], sq_rot, psS, sd, sd_b, 1.0 / D, 0, KC)

                def norm_scale(tt):
                    i = tt % 2
                    for k in range(KC):
                        sc.op(dve, lambda k=k: nc.vector.scalar_tensor_tensor(
                            out=ht[:, k, :], in0=xts[i][:, k, :], scalar=prm[:, gcol + k:gcol + k + 1], in1=sd[:],
                            op0=ALU.mult, op1=ALU.mult), reads=[xt_b[i], sd_b, prm_b], writes=[ht_b])

                load_x(0)
                norm_stats(0)
                norm_scale(0)
                for tt in range(NT):
                    i = tt % 2
                    if tt + 1 < NT:
                        load_x(tt + 1)
                    for f in range(NF):
                        (pg, pgb), (pu, pub) = G_rot.get(), U_rot.get()
                        for k in range(KC):
                            mm(pg[:], wgu[:, f, 0, k, :], ht[:, k, :], k == 0, k == KC - 1, [wgu_b[f], ht_b], [pgb])
                        for k in range(KC):
                            mm(pu[:], wgu[:, f, 1, k, :], ht[:, k, :], k == 0, k == KC - 1, [wgu_b[f], ht_b], [pub])
                        sg, sg_b = sg_rot.get()
                        sc.op(act, lambda: nc.scalar.activation(out=sg[:], in_=pg[:], func=AF.Silu), reads=[pgb], writes=[sg_b])
                        sc.op(dve, lambda: nc.vector.tensor_tensor(out=At[:, f, :], in0=sg[:], in1=pu[:], op=ALU.mult),
                              reads=[sg_b, pub], writes=[A_b[f]])
                        if f == 12 and tt + 1 < NT:
                            norm_stats(tt + 1)
                    if tt + 1 < NT:
                        norm_scale(tt + 1)
                    for d in range(KC):
                        py, pyb = Y_rot.get()
                        for f in range(NF):
                            mm(py[:], wdt[:, f, d * 128:(d + 1) * 128], At[:, f, :], f == 0, f == NF - 1, [wd_b[f], A_b[f]], [pyb])
                        sc.op(dve, lambda d=d: nc.vector.scalar_tensor_tensor(
                            out=xts[i][:, d, :], in0=py[:], scalar=0.5, in1=xts[i][:, d, :], op0=ALU.mult, op1=ALU.add),
                            reads=[pyb, xt_b[i]], writes=[xt_b[i]])
                    sc.dma("sp", y_view[:, :, tt * TT:(tt + 1) * TT], xts[i][:], reads=[xt_b[i]], writes=[xd_b[tt]])
                sc.barrier()

        def mixer_phase(l, upto=None, m4_hooks=None):
            with ExitStack() as st:
                hT = sb("hT", [128, KC, S], BF16, st)
                hT_b = [Buf(f"hT{t}") for t in range(NT)]
                hT_flat = hT[:].rearrange("p k t -> p (k t)")
                wrot = Rot([(sb(f"wp{i}", [128, KC, 128], BF16, st), Buf(f"wp{i}")) for i in range(6)])
                P_rot = Rot([PS[0], PS[1]])

                with ExitStack() as s1:
                    xts = [sb(f"mx{i}", [128, KC, TT], F32, s1) for i in range(2)]
                    xt_b = [Buf(f"mx{i}") for i in range(2)]
                    sq_rot = Rot([(sb(f"msq{i}", [128, TT], BF16, s1), Buf(f"msq{i}")) for i in range(3)])
                    sd_rot = Rot([(sb(f"msd{i}", [128, TT], F32, s1), Buf(f"msd{i}")) for i in range(2)])
                    gcol = P_GM + l * KC
                    sc.dma("sp", xts[0][:], y_view[:, :, 0:TT], reads=[xd_b[0]], writes=[xt_b[0]])
                    for tt in range(NT):
                        i = tt % 2
                        if tt + 1 < NT:
                            sc.dma("sp", xts[(tt + 1) % 2][:], y_view[:, :, (tt + 1) * TT:(tt + 2) * TT],
                                   reads=[xd_b[tt + 1]], writes=[xt_b[(tt + 1) % 2]])
                        sd, sd_b = sd_rot.get()
                        rms_stats([xts[i][:, k, :] for k in range(KC)], xt_b[i], sq_rot, PS[6 + (tt % 2)], sd, sd_b, 1.0 / D, 0, KC)
                        for k in range(KC):
                            sc.op(dve, lambda k=k: nc.vector.scalar_tensor_tensor(
                                out=hT[:, k, tt * TT:(tt + 1) * TT], in0=xts[i][:, k, :], scalar=prm[:, gcol + k:gcol + k + 1],
                                in1=sd[:], op0=ALU.mult, op1=ALU.mult), reads=[xt_b[i], sd_b, prm_b], writes=[hT_b[tt]])
                    sc.barrier()
                    if upto == "m1":
                        return

                def load_w(blk):
                    w, w_b = wrot.get()
                    sc.dma("pool", w[:], win_d[l, blk], writes=[w_b])
                    return w, w_b

                def proj_tile(w, w_b, tt, ps):
                    pst, psb = ps
                    for k in range(KC):
                        mm(pst[:], w[:, k, :], hT[:, k, tt * TT:(tt + 1) * TT], k == 0, k == KC - 1, [w_b, hT_b[tt]], [psb])

                with ExitStack() as s2:
                    NW = S + 4
                    HS = S // 2
                    LX = [sb(f"llx{i}", [128, NW], F32, s2) for i in range(2)]
                    LX_b = [[Buf(f"llx{i}h{h}") for h in range(2)] for i in range(2)]
                    GL = [sb(f"lgl{i}", [128, S], BF16, s2) for i in range(2)]
                    GL_b = [Buf(f"lgl{i}") for i in range(2)]
                    XC = sb("lxc", [128, S], F32, s2)
                    XC_b = [Buf(f"lxc{h}") for h in range(2)]
                    IG = sb("lig", [128, S], F32, s2)
                    IG_b = [Buf(f"lig{h}") for h in range(2)]
                    PP = sb("lpp", [128, S], F32, s2)
                    PP_b = [Buf(f"lpp{h}") for h in range(2)]
                    xcb = sb("xcb", [128, S], BF16, s2)
                    xcb_b = [Buf(f"xcb{h}") for h in range(2)]
                    ybf = [sb(f"ybf{i}", [128, S], BF16, s2) for i in range(2)]
                    ybf_b = [Buf(f"ybf{i}") for i in range(2)]
                    bd_rot = Rot([(sb(f"bd{i}", [128, 128], BF16, s2), Buf(f"bd{i}")) for i in range(4)])
                    L_rot = Rot(PS)
                    for i in range(2):
                        sc.op(dve, lambda i=i: nc.vector.memset(LX[i][:, 0:3], 0.0), writes=[LX_b[i][0]])

                    def lx_mm(c):
                        wx, wx_b = load_w(18 + c)
                        tiles = []
                        for tt in range(NT):
                            ps = L_rot.get()
                            proj_tile(wx, wx_b, tt, ps)
                            tiles.append(ps)
                        return tiles

                    def lx_evac(c, tiles, tts):
                        for tt in tts:
                            ps = tiles[tt]
                            sc.op(act, lambda: nc.scalar.copy(out=LX[c % 2][:, 3 + tt * TT:3 + (tt + 1) * TT], in_=ps[0][:]),
                                  reads=[ps[1]], writes=[LX_b[c % 2][tt // 4]])

                    def gl_proj(c):
                        wgt, wgt_b = load_w(24 + c)
                        for tt in range(NT):
                            ps = L_rot.get()
                            proj_tile(wgt, wgt_b, tt, ps)
                            sc.op(act, lambda: nc.scalar.activation(out=GL[c % 2][:, tt * TT:(tt + 1) * TT], in_=ps[0][:],
                                                                    func=AF.Gelu_apprx_tanh), reads=[ps[1]], writes=[GL_b[c % 2]])

                    t0_ = lx_mm(0)
                    lx_evac(0, t0_, range(NT))
                    gl_proj(0)
                    for c in range(NLC):
                        pc = l * NLC + c
                        lxt, lxb = LX[c % 2], LX_b[c % 2]
                        bda, bda_b = bd_rot.get()
                        bdx, bdx_b = bd_rot.get()
                        sc.dma("pool", bda[:], bda_d[l, c], writes=[bda_b])
                        sc.dma("pool", bdx[:], bdx_d[l, c], writes=[bdx_b])
                        cw0 = P_CW + pc * 4
                        hc = [slice(h * HS, (h + 1) * HS) for h in range(2)]
                        zd = [lxt[:, 3 + h * HS:3 + (h + 1) * HS] for h in range(2)]
                        for h in range(2):
                            sc.op(act, lambda h=h: nc.scalar.activation(out=XC[:, hc[h]], in_=lxt[:, h * HS:h * HS + HS], func=AF.Identity,
                                                                        scale=prm[:, cw0:cw0 + 1], bias=prm[:, P_CB + pc:P_CB + pc + 1]),
                                  reads=[lxb[0], prm_b] + ([lxb[1]] if h == 1 else []), writes=[XC_b[h]])
                        for h in range(2):
                            lr = [lxb[0]] + ([lxb[1]] if h == 1 else [])
                            for j in range(1, 4):
                                sc.op(dve, lambda j=j, h=h: nc.vector.scalar_tensor_tensor(
                                    out=XC[:, hc[h]], in0=lxt[:, h * HS + j:h * HS + j + HS], scalar=prm[:, cw0 + j:cw0 + j + 1], in1=XC[:, hc[h]],
                                    op0=ALU.mult, op1=ALU.add), reads=lr + [XC_b[h], prm_b], writes=[XC_b[h]])
                        for h in range(2):
                            sc.op(act, lambda h=h: nc.scalar.copy(out=xcb[:, hc[h]], in_=XC[:, hc[h]]), reads=[XC_b[h]], writes=[xcb_b[h]])
                            for tt in range(4 * h, 4 * h + 4):
                                ps = L_rot.get()
                                mm(ps[0][:], bda[:], xcb[:, tt * TT:(tt + 1) * TT], True, True, [bda_b, xcb_b[h]], [ps[1]])
                                sc.op(act, lambda: nc.scalar.activation(out=lxt[:, 3 + tt * TT:3 + (tt + 1) * TT], in_=ps[0][:], func=AF.Sigmoid,
                                                                        bias=prm[:, P_GA + pc:P_GA + pc + 1]),
                                      reads=[ps[1], prm_b], writes=[lxb[h]])
                            for tt in range(4 * h, 4 * h + 4):
                                ps = L_rot.get()
                                mm(ps[0][:], bdx[:], xcb[:, tt * TT:(tt + 1) * TT], True, True, [bdx_b, xcb_b[h]], [ps[1]])
                                sc.op(act, lambda: nc.scalar.activation(out=IG[:, tt * TT:(tt + 1) * TT], in_=ps[0][:], func=AF.Sigmoid,
                                                                        bias=prm[:, P_GX + pc:P_GX + pc + 1]),
                                      reads=[ps[1], prm_b], writes=[IG_b[h]])
                        nxt_tiles = lx_mm(c + 1) if c + 1 < NLC else None
                        for h in range(2):
                            r_ = zd[h]
                            p_ = PP[:, hc[h]]
                            sc.op(dve, lambda: nc.vector.tensor_scalar(out=p_, in0=r_, scalar1=ckt[:, 4, pc:pc + 1], scalar2=ckt[:, 3, pc:pc + 1],
                                                                       op0=ALU.mult, op1=ALU.add), reads=[lxb[h], ck_b], writes=[PP_b[h]])
                            sc.op(dve, lambda: nc.vector.tensor_tensor(out=p_, in0=p_, in1=r_, op=ALU.mult), reads=[PP_b[h], lxb[h]], writes=[PP_b[h]])
                            for k_ in (2, 1, 0):
                                sc.op(dve, lambda k_=k_: nc.vector.scalar_tensor_tensor(out=p_, in0=p_, scalar=ckt[:, k_, pc:pc + 1], in1=r_,
                                                                                        op0=ALU.add, op1=ALU.mult),
                                      reads=[PP_b[h], lxb[h], ck_b], writes=[PP_b[h]])
                            sc.op(act, lambda: nc.scalar.add(out=r_, in_=p_, add=1.0), reads=[PP_b[h]], writes=[lxb[h]])
                            sc.op(dve, lambda: nc.vector.scalar_tensor_tensor(out=p_, in0=p_, scalar=2.0, in1=p_, op0=ALU.add, op1=ALU.mult),
                                  reads=[PP_b[h]], writes=[PP_b[h]])
                            sc.op(act, lambda: nc.scalar.activation(out=p_, in_=p_, func=AF.Sqrt, scale=-1.0), reads=[PP_b[h]], writes=[PP_b[h]])
                            if nxt_tiles is not None:
                                lx_evac(c + 1, nxt_tiles, range(4 * h, 4 * h + 4))
                        if c + 1 < NLC:
                            gl_proj(c + 1)
                        for h in range(2):
                            sc.op(dve, lambda h=h: nc.vector.tensor_tensor(out=IG[:, hc[h]], in0=IG[:, hc[h]], in1=XC[:, hc[h]], op=ALU.mult),
                                  reads=[IG_b[h], XC_b[h]], writes=[IG_b[h]])
                            sc.op(dve, lambda h=h: nc.vector.tensor_tensor(out=IG[:, hc[h]], in0=IG[:, hc[h]], in1=PP[:, hc[h]], op=ALU.mult),
                                  reads=[IG_b[h], PP_b[h]], writes=[IG_b[h]])
                        sc.op(dve, lambda: nc.vector.tensor_tensor_scan(out=XC[:], data0=lxt[:, 3:3 + S], data1=IG[:], initial=0.0,
                                                                        op0=ALU.mult, op1=ALU.add),
                              reads=[lxb[0], lxb[1], IG_b[0], IG_b[1], XC_b[0], XC_b[1]], writes=[XC_b[0], XC_b[1]])
                        yb, yb_b = ybf[c % 2], ybf_b[c % 2]
                        for h in range(2):
                            sc.op(dve, lambda h=h: nc.vector.tensor_tensor(out=yb[:, hc[h]], in0=XC[:, hc[h]], in1=GL[c % 2][:, hc[h]], op=ALU.mult),
                                  reads=[XC_b[h], GL_b[c % 2]], writes=[yb_b])
                        sc.dma("sp", mixs[2 + c], yb[:], reads=[yb_b], writes=[mixd_b[2 + c]])
                        if upto == "lru%d" % c:
                            sc.barrier()
                            return
                    sc.barrier()
                    if upto == "lru":
                        return

                with ExitStack() as s3:
                    acc = [sb(f"acc{i}", [128, S], F32, s3) for i in range(2)]
                    acc_b = [Buf(f"acc{i}") for i in range(2)]
                    qp = [sb(f"qp{s_}", [128, 32, 256], BF16, s3) for s_ in range(2)]
                    qp_b = [Buf(f"qp{s_}") for s_ in range(2)]
                    kTs = [sb(f"kT{s_}", [128, S], BF16, s3) for s_ in range(2)]
                    kT_bs = [Buf(f"kT{s_}") for s_ in range(2)]
                    va = [sb(f"va{s_}", [128, 32, 192], BF16, s3) for s_ in range(2)]
                    va_b = [Buf(f"va{s_}") for s_ in range(2)]
                    em_rot = Rot([(sb(f"em{i}", [128, 512], F32, s3), Buf(f"em{i}")) for i in range(2)])
                    ex_rot = Rot([(sb(f"ex{i}", [128, 512], F32, s3), Buf(f"ex{i}")) for i in range(2)])
                    pt_rot = Rot([(sb(f"pt{i}", [128, 512], BF16, s3), Buf(f"pt{i}")) for i in range(3)])
                    sq_rot = Rot([(sb(f"asq{i}", [128, TT], BF16, s3), Buf(f"asq{i}")) for i in range(2)])
                    sd_rot = Rot([(sb(f"asd{i}", [128, TT], F32, s3), Buf(f"asd{i}")) for i in range(2)])
                    tmp_rot = Rot([(sb(f"atm{i}", [128, TT], F32, s3), Buf(f"atm{i}")) for i in range(1)])
                    att_rot = Rot([(sb(f"att{i}", [128, TT], BF16, s3), Buf(f"att{i}")) for i in range(2)])
                    for s_ in range(2):
                        sc.op(dve, lambda s_=s_: nc.vector.memset(va[s_][:, :, 64:128], 1.0), writes=[va_b[s_]])
                        sc.op(dve, lambda s_=s_: nc.vector.memset(qp[s_][64:128, :, 0:128], 0.0), writes=[qp_b[s_]])
                        sc.op(dve, lambda s_=s_: nc.vector.memset(qp[s_][0:64, :, 128:256], 0.0), writes=[qp_b[s_]])
                    ST_rot = Rot([PS[4], PS[5], PS[6]])
                    A_rot = Rot([PS[0], PS[1], PS[2]])
                    PSTAT = PS[3]
                    po, po_b = PS[7]
                    em_of = {}

                    def proj_gen(hp, g, s_):
                        win_, r = GROUPS[g]
                        nb = (S // r) // 128
                        kT, kT_b = kTs[s_], kT_bs[s_]
                        vA, vA_b = va[s_], va_b[s_]
                        emt, emt_b = em_rot.get()
                        sc.dma("sp", emt[:], emask_d[g * 2 + hp], writes=[emt_b])
                        em_of[(hp, g)] = (emt, emt_b)
                        wq_ = load_w(2 * g + hp)
                        wk_ = load_w(6 + 2 * g + hp)
                        items = [(isq, tt) for isq in range(2) for tt in range(NT)]

                        def stepA(isq, tt):
                            w, w_b = (wq_, wk_)[isq]
                            ps = A_rot.get()
                            proj_tile(w, w_b, tt, ps)
                            return ps

                        def stepB(isq, tt, ps):
                            ncol = (P_QN + l, P_KN + l)[isq]
                            sd, sd_b = sd_rot.get()
                            rms_stats([ps[0][:]], ps[1], sq_rot, PSTAT, sd, sd_b, 1.0 / 64.0, 1, 1, lnexp=True)
                            if isq == 1:
                                if r == 1:
                                    o_ap = kT[:, tt * TT:(tt + 1) * TT]
                                    i_ap = ps[0][:]
                                    d_ap = sd[:]
                                else:
                                    na = TT // r
                                    o_ap = kT[:].rearrange("p (c l) -> p c l", c=r)[:, :, tt * na:(tt + 1) * na]
                                    i_ap = ps[0][:].rearrange("p (a c) -> p c a", c=r)
                                    d_ap = sd[:].rearrange("p (a c) -> p c a", c=r)
                                sc.op(dve, lambda: nc.vector.scalar_tensor_tensor(
                                    out=o_ap, in0=i_ap, scalar=prm[:, ncol:ncol + 1], in1=d_ap, op0=ALU.mult, op1=ALU.mult),
                                    reads=[ps[1], sd_b, prm_b], writes=[kT_b])
                            else:
                                for hh in range(2):
                                    rows = slice(hh * 64, hh * 64 + 64)
                                    hoff = hh * 128
                                    if r == 1:
                                        o_ap = qp[s_][rows, 4 * tt:4 * tt + 4, hoff:hoff + 128]
                                        i_ap = ps[0][rows, :].rearrange("p (j i) -> p j i", j=4)
                                        d_ap = sd[rows, :].rearrange("p (j i) -> p j i", j=4)
                                    elif r == 4:
                                        o_ap = qp[s_][rows, :, :].rearrange("p (c t) e -> p c t e", c=4)[:, :, tt, hoff:hoff + 128]
                                        i_ap = ps[0][rows, :].rearrange("p (a c) -> p c a", c=4)
                                        d_ap = sd[rows, :].rearrange("p (a c) -> p c a", c=4)
                                    else:
                                        i0 = hoff + 32 * (tt % 4)
                                        o_ap = qp[s_][rows, :, :].rearrange("p (c t) e -> p c t e", c=16)[:, :, tt // 4, i0:i0 + 32]
                                        i_ap = ps[0][rows, :].rearrange("p (a c) -> p c a", c=16)
                                        d_ap = sd[rows, :].rearrange("p (a c) -> p c a", c=16)
                                    sc.op(dve, lambda: nc.vector.scalar_tensor_tensor(
                                        out=o_ap, in0=i_ap, scalar=prm[rows, ncol:ncol + 1], in1=d_ap, op0=ALU.mult, op1=ALU.mult),
                                        reads=[ps[1], sd_b, prm_b], writes=[qp_b[s_]])

                        nxt_ps = stepA(*items[0])
                        for idx, it_ in enumerate(items):
                            cur_ps = nxt_ps
                            if idx + 1 < len(items):
                                nxt_ps = stepA(*items[idx + 1])
                            stepB(it_[0], it_[1], cur_ps)
                            yield
                        wv, wv_b = load_w(12 + 2 * g + hp)
                        for b4 in range(8):
                            pv, pvb = A_rot.get()
                            for j in range(4):
                                blk = b4 * 4 + j
                                c_, kb = divmod(blk, nb)
                                t0 = c_ + r * 128 * kb
                                tts = list(range(t0 // TT, (t0 + r * 127) // TT + 1))
                                for k in range(KC):
                                    base = k * S + t0
                                    mm(pv[:, j * 128:(j + 1) * 128], hT_flat[:, base:base + 127 * r + 1:r], wv[:, k, :], k == 0, k == KC - 1,
                                       [hT_b[t_] for t_ in tts] + [wv_b], [pvb])
                            pv3 = pv[:].rearrange("p (j e) -> p j e", j=4)
                            sc.op(act, lambda: nc.scalar.copy(out=vA[:, b4 * 4:b4 * 4 + 4, 0:64], in_=pv3[:, :, 0:64]),
                                  reads=[pvb], writes=[vA_b])
                            sc.op(act, lambda: nc.scalar.copy(out=vA[:, b4 * 4:b4 * 4 + 4, 128:192], in_=pv3[:, :, 64:128]),
                                  reads=[pvb], writes=[vA_b])
                            yield

                    LOOK = 2
                    PBURST = 8

                    def attn_gen(hp, g, s_):
                        win_, r = GROUPS[g]
                        nb = (S // r) // 128
                        kT, kT_b = kTs[s_], kT_bs[s_]
                        vA, vA_b = va[s_], va_b[s_]
                        emt, emt_b = em_of[(hp, g)]

                        def qk_block(blk):
                            c_, qb = divmod(blk, nb)
                            cur = slice(blk * 128, (blk + 1) * 128)
                            prv = slice((blk - 1) * 128, blk * 128)
                            stt, stb = ST_rot.get()
                            mm(stt[:, 0:256], kT[:, cur], qp[s_][:, blk, :], True, True, [kT_b, qp_b[s_]], [stb])
                            if qb > 0:
                                mm(stt[:, 256:512], kT[:, prv], qp[s_][:, blk, :], True, True, [kT_b, qp_b[s_]], [stb])
                            return stt, stb

                        pend = [qk_block(b_) for b_ in range(LOOK)]
                        for blk in range(32):
                            stt, stb = pend.pop(0)
                            if blk + LOOK < 32:
                                pend.append(qk_block(blk + LOOK))
                            c_, qb = divmod(blk, nb)
                            j = blk % 2
                            ex, ex_b = ex_rot.get()
                            sc.op(act, lambda: nc.scalar.activation(out=ex[:], in_=stt[:], func=AF.Exp, scale=0.125),
                                  reads=[stb], writes=[ex_b])
                            pt, pt_b = pt_rot.get()
                            sc.op(dve, lambda: nc.vector.tensor_tensor(out=pt[:], in0=ex[:], in1=emt[:], op=ALU.mult),
                                  reads=[ex_b, emt_b], writes=[pt_b])
                            for hh in range(2):
                                vc = slice(hh * 64, hh * 64 + 128)
                                oc = slice(hh * 256 + j * 128, hh * 256 + (j + 1) * 128)
                                mm(po[:, oc], vA[:, blk, vc], pt[:, hh * 128:hh * 128 + 128], True, qb == 0, [vA_b, pt_b], [po_b])
                                if qb > 0:
                                    mm(po[:, oc], vA[:, blk - 1, vc], pt[:, 256 + hh * 128:256 + hh * 128 + 128], False, True, [vA_b, pt_b], [po_b])
                            if j == 1:
                                blk0 = blk - 1
                                c0, qb0 = divmod(blk0, nb)
                                start = c0 + r * 128 * qb0
                                for hh in range(2):
                                    a_ap = acc[hh][:, start:start + 255 * r + 1:r]
                                    p_ap = po[:, hh * 256:(hh + 1) * 256]
                                    if g == 0:
                                        sc.op(dve, lambda: nc.vector.tensor_copy(out=a_ap, in_=p_ap), reads=[po_b], writes=[acc_b[hh]])
                                    else:
                                        sc.op(dve, lambda: nc.vector.tensor_tensor(out=a_ap, in0=a_ap, in1=p_ap, op=ALU.add),
                                              reads=[po_b, acc_b[hh]], writes=[acc_b[hh]])
                            yield

                    def normalize(hp):
                        for tt in range(NT):
                            cols = slice(tt * TT, (tt + 1) * TT)
                            tm, tm_b = tmp_rot.get()
                            at, at_b = att_rot.get()
                            sc.op(dve, lambda: nc.vector.tensor_copy(out=tm[0:64, :], in_=acc[0][64:128, cols]), reads=[acc_b[0]], writes=[tm_b])
                            sc.op(dve, lambda: nc.vector.tensor_copy(out=tm[64:128, :], in_=acc[1][0:64, cols]), reads=[acc_b[1]], writes=[tm_b])
                            sc.op(dve, lambda: nc.vector.reciprocal(out=tm[:], in_=tm[:]), reads=[tm_b], writes=[tm_b])
                            sc.op(dve, lambda: nc.vector.tensor_tensor(out=at[0:64, :], in0=acc[0][0:64, cols], in1=tm[0:64, :], op=ALU.mult),
                                  reads=[acc_b[0], tm_b], writes=[at_b])
                            sc.op(dve, lambda: nc.vector.tensor_tensor(out=at[64:128, :], in0=acc[1][64:128, cols], in1=tm[64:128, :], op=ALU.mult),
                                  reads=[acc_b[1], tm_b], writes=[at_b])
                            sc.dma("sp", mixs[hp][:, cols], at[:], reads=[at_b], writes=[mixd_b[hp]])

                    seq = [(hp, g) for hp in range(2) for g in range(3)]
                    for _ in proj_gen(seq[0][0], seq[0][1], 0):
                        pass
                    for i_, (hp, g) in enumerate(seq):
                        A = attn_gen(hp, g, i_ % 2)
                        P = proj_gen(seq[i_ + 1][0], seq[i_ + 1][1], (i_ + 1) % 2) if i_ + 1 < len(seq) else None
                        n_p = 24
                        done_p = 0
                        for step, _ in enumerate(A):
                            if P is not None:
                                want = ((((step + 1) * n_p) // 32) // PBURST) * PBURST
                                while done_p < want:
                                    next(P, None)
                                    done_p += 1
                        if P is not None:
                            for _ in P:
                                pass
                        if g == 2:
                            normalize(hp)
                    sc.barrier()
                    if upto == "attn":
                        return
            if m4_hooks is not None:
                m4_hooks[0]()
            with ExitStack() as s4:
                wo = sb("wo", [128, KC, D], BF16, s4)
                wo_b = [Buf(f"wo{k}") for k in range(KC)]
                xts = [sb(f"ox{i}", [128, KC, TT], F32, s4) for i in range(2)]
                xt_b = [Buf(f"ox{i}") for i in range(2)]
                mts = [sb(f"om{i}", [128, KC, TT], BF16, s4) for i in range(2)]
                mt_b = [Buf(f"om{i}") for i in range(2)]
                Y_rot = Rot([PS[0], PS[1], PS[2], PS[3]])
                for k in range(KC):
                    sc.dma("pool", wo[:, k, :], wout_d[l, k], writes=[wo_b[k]])
                if m4_hooks is not None:
                    m4_hooks[1]()

                def load(tt):
                    i = tt % 2
                    sc.dma("sp", xts[i][:], y_view[:, :, tt * TT:(tt + 1) * TT], reads=[xd_b[tt]], writes=[xt_b[i]])
                    sc.dma("sp", mts[i][:], mix_view[:, :, tt * TT:(tt + 1) * TT], reads=mixd_b, writes=[mt_b[i]])

                load(0)
                for tt in range(NT):
                    i = tt % 2
                    if tt + 1 < NT:
                        load(tt + 1)
                    for d in range(KC):
                        py, pyb = Y_rot.get()
                        for k in range(KC):
                            mm(py[:], wo[:, k, d * 128:(d + 1) * 128], mts[i][:, k, :], k == 0, k == KC - 1, [wo_b[k], mt_b[i]], [pyb])
                        sc.op(dve, lambda d=d: nc.vector.tensor_tensor(out=xts[i][:, d, :], in0=py[:], in1=xts[i][:, d, :], op=ALU.add),
                              reads=[pyb, xt_b[i]], writes=[xt_b[i]])
                    sc.dma("sp", y_view[:, :, tt * TT:(tt + 1) * TT], xts[i][:], reads=[xt_b[i]], writes=[xd_b[tt]])
                sc.barrier()

        done = False
        for l in range(n_layers):
            ffn_phase(l, 0, x_view_in if l == 0 else y_view, first=(l == 0))
            if stop_after == (l, "ffn1"):
                done = True
                break
            if stop_after is not None and stop_after[0] == l and len(stop_after) == 3:
                mixer_phase(l, stop_after[2])
                break
            if stop_after == (l, "mixer"):
                mixer_phase(l)
                done = True
                break
            with ExitStack() as wst:
                hold = {}

                def _alloc():
                    hold["W"] = ffn_alloc(wst)

                def _load(l=l):
                    ffn_load(hold["W"], l, 1)

                if PREFETCH_FFN2:
                    mixer_phase(l, m4_hooks=(_alloc, _load))
                    ffn_phase(l, 1, y_view, first=False, W=hold["W"])
                else:
                    mixer_phase(l)
                    ffn_phase(l, 1, y_view, first=False)
        sc.barrier()
        build_program.stats = dict(cnt={n: e.cnt for n, e in sc.engs.items()}, waits=sc.n_wait)
    return nc


def _slopes():
    h = np.arange(1, 13, dtype=np.float64)
    return np.exp2(-8.0 * h / 12.0).reshape(3, 4)


def _emask():
    sl = _slopes()
    i = np.arange(128)[:, None].astype(np.float64)
    j = np.arange(128)[None, :].astype(np.float64)
    out = np.zeros((6, 128, 512), np.float32)
    for g, (_, r) in enumerate(GROUPS):
        for hp in range(2):
            for hh in range(2):
                a = sl[g, 2 * hp + hh] * r
                cur = np.where(j >= i, np.exp(-a * (j - i)), 0.0)
                prv = np.where(i >= j, np.exp(-a * (128.0 + j - i)), 0.0)
                out[g * 2 + hp, :, hh * 128:hh * 128 + 128] = cur
                out[g * 2 + hp, :, 256 + hh * 128:256 + hh * 128 + 128] = prv
    return out


def _prep_weights(inp):
    f32 = np.float32
    c = np.ascontiguousarray
    out = {}
    for i, nm in ((1, "ffn1"), (2, "ffn2")):
        for key, src in ((f"wg{i}", f"{nm}_w_gate"), (f"wu{i}", f"{nm}_w_up")):
            w = np.asarray(inp[src], f32).reshape(DEPTH, KC, 128, NF, 128)
            out[key] = c(w.transpose(0, 3, 2, 1, 4))
        out[f"wd{i}"] = c(np.asarray(inp[f"{nm}_w_down"], f32).reshape(DEPTH, NF, 128, D))
    w = np.asarray(inp["w_in"], f32).reshape(DEPTH, KC, 128, 30, 128)
    out["win"] = c(w.transpose(0, 3, 2, 1, 4))
    out["wout"] = c(np.asarray(inp["w_out"], f32).reshape(DEPTH, KC, 128, D))
    for key, src in (("bda", "gate_a_w"), ("bdx", "gate_x_w")):
        gw = np.asarray(inp[src], f32)
        bd = np.zeros((DEPTH, NLC, 128, 128), f32)
        for cidx in range(NLC):
            bd[:, cidx, 0:64, 0:64] = gw[:, 2 * cidx]
            bd[:, cidx, 64:128, 64:128] = gw[:, 2 * cidx + 1]
        out[key] = bd
    prm = np.zeros((128, NPRM), f32)
    for base, src in ((P_G1, "ffn1_norm"), (P_GM, "mix_norm"), (P_G2, "ffn2_norm")):
        g_ = np.asarray(inp[src], f32).reshape(DEPTH, KC, 128)
        prm[:, base:base + DEPTH * KC] = g_.transpose(2, 0, 1).reshape(128, DEPTH * KC)
    prm[:, P_QN:P_QN + DEPTH] = np.tile(np.asarray(inp["q_norm"], f32).T, (2, 1))
    prm[:, P_KN:P_KN + DEPTH] = np.tile(np.asarray(inp["k_norm"], f32).T, (2, 1))
    cw = np.asarray(inp["conv_w"], f32).reshape(DEPTH, 4, NLC, 128)
    prm[:, P_CW:P_CW + DEPTH * NLC * 4] = cw.transpose(3, 0, 2, 1).reshape(128, DEPTH * NLC * 4)
    for base, src in ((P_CB, "conv_b"), (P_GA, "gate_a_b"), (P_GX, "gate_x_b"), (P_LM, "lru_lambda")):
        v = np.asarray(inp[src], f32).reshape(DEPTH, NLC, 128)
        prm[:, base:base + DEPTH * NLC] = v.transpose(2, 0, 1).reshape(128, DEPTH * NLC)
    out["prm"] = prm
    out["emask"] = _emask()
    ob = np.zeros((2, 128, 128), f32)
    ob[0] = 1.0
    ob[1, 0:64, 0:64] = 1.0
    ob[1, 64:128, 64:128] = 1.0
    out["onesbd"] = ob
    return out


def _run(inp, cores, n_layers=DEPTH, stop_after=None, trace=False):
    shared = _prep_weights(inp)
    x = np.asarray(inp["x"], np.float32)
    nc = build_program(n_layers, stop_after)
    in_maps = []
    for b in cores:
        m = dict(shared)
        m["xT"] = np.ascontiguousarray(x[b].T).reshape(KC, 128, S)
        in_maps.append(m)
    res = run_bass_kernel_spmd(nc, in_maps, core_ids=list(range(len(cores))), trace=trace)
    outs = [np.ascontiguousarray(r["y"].reshape(D, S).T) for r in res.results]
    return outs, res


def kernel(**inputs):
    outs, _ = _run(inputs, list(range(NCORES)))
    return np.stack(outs, axis=0).astype(np.float32)
```

```python
import math
from contextlib import ExitStack

import numpy as np
import concourse.bass as bass
import concourse.mybir as mybir
from concourse.bass_utils import run_bass_kernel_spmd

F32 = mybir.dt.float32
BF16 = mybir.dt.bfloat16
AF = mybir.ActivationFunctionType
ALU = mybir.AluOpType

D = 1024
S = 4096
DEPTH = 4
DFF = 2816
NF = DFF // 128
KC = D // 128
TT = 512
NT = S // TT
DLRU = 768
NLC = DLRU // 128
GROUPS = ((128, 1), (512, 4), (2048, 16))
EPS = 1e-6
NCORES = 8
PREFETCH_FFN2 = False

P_G1 = 0
P_GM = P_G1 + DEPTH * KC
P_G2 = P_GM + DEPTH * KC
P_QN = P_G2 + DEPTH * KC
P_KN = P_QN + DEPTH
P_CW = P_KN + DEPTH
P_CB = P_CW + DEPTH * NLC * 4
P_GA = P_CB + DEPTH * NLC
P_GX = P_GA + DEPTH * NLC
P_LM = P_GX + DEPTH * NLC
NPRM = P_LM + DEPTH * NLC


class Eng:
    def __init__(self, name, h, sem):
        self.name = name
        self.h = h
        self.sem = sem
        self.cnt = 0
        self.pending = False
        self.known = {}


class Buf:
    __slots__ = ("name", "w", "wd", "r", "rd")

    def __init__(self, name):
        self.name = name
        self.w = {}
        self.wd = []
        self.r = {}
        self.rd = []


class Sched:
    KRING = 8

    def __init__(self, nc, stack):
        self.nc = nc

        def mk(n, h):
            return Eng(n, h, stack.enter_context(nc.semaphore("sem_" + n)))

        self.pe = mk("pe", nc.tensor)
        self.act = mk("act", nc.scalar)
        self.dve = mk("dve", nc.vector)
        self.pool = mk("pool", nc.gpsimd)
        self.sp = mk("sp", nc.sync)
        self.engs = {e.name: e for e in (self.pe, self.act, self.dve, self.pool, self.sp)}
        self.rings = {}
        for q in ("sp", "pool"):
            sems = [stack.enter_context(nc.semaphore(f"dma_{q}{i}")) for i in range(self.KRING)]
            self.rings[q] = {"sems": sems, "vals": [0] * self.KRING, "i": 0}
        self.n_wait = 0

    def _wait(self, E, tok):
        if tok[0] == "e":
            _, n, v = tok
            if E.known.get(n, 0) >= v:
                return
            E.h.wait_ge(self.engs[n].sem, v)
            E.known[n] = v
        else:
            _, sem, key, v = tok
            if E.known.get(key, 0) >= v:
                return
            E.h.wait_ge(sem, v)
            E.known[key] = v
        self.n_wait += 1

    def _deps(self, E, reads, writes):
        toks = []
        for b in reads:
            for n, v in b.w.items():
                if n == E.name and n == "pe":
                    continue
                toks.append(("e", n, v))
            toks.extend(b.wd)
        for b in writes:
            if b.r or b.rd:
                for n, v in b.r.items():
                    if n == E.name and n == "pe":
                        continue
                    toks.append(("e", n, v))
                toks.extend(b.rd)
        return toks

    def _commit(self, E, reads, writes, val=None, tok=None):
        for b in writes:
            if b.r or b.rd:
                b.w = {}
                b.wd = []
                b.r = {}
                b.rd = []
            if tok is None:
                b.w[E.name] = val
            else:
                b.wd.append(tok)
        for b in reads:
            if tok is None:
                b.r[E.name] = val
            else:
                b.rd.append(tok)

    def op(self, E, emit, reads=(), writes=(), inc=True):
        for t in self._deps(E, reads, writes):
            self._wait(E, t)
        ins = emit()
        if inc:
            E.cnt += 1
            ins.then_inc(E.sem, 1)
            val = E.cnt
            E.pending = False
        else:
            val = E.cnt + 1
            E.pending = True
        self._commit(E, reads, writes, val=val)
        return ins

    def dma(self, q, out, in_, reads=(), writes=()):
        E = self.engs[q]
        ring = self.rings[q]
        for t in self._deps(E, reads, writes):
            self._wait(E, t)
        slot = ring["i"] % self.KRING
        ring["i"] += 1
        sem = ring["sems"][slot]
        key = (q, slot)
        if ring["vals"][slot] > 0:
            self._wait(E, ("d", sem, key, ring["vals"][slot]))
        E.h.dma_start(out=out, in_=in_).then_inc(sem, 16)
        ring["vals"][slot] += 16
        tok = ("d", sem, key, ring["vals"][slot])
        self._commit(E, reads, writes, tok=tok)

    def barrier(self):
        assert not self.pe.pending
        for E in self.engs.values():
            for F in self.engs.values():
                if F.cnt > 0 and not (F is E and E.name in ("pe", "sp")):
                    self._wait(E, ("e", F.name, F.cnt))
            for q, ring in self.rings.items():
                for slot in range(self.KRING):
                    if ring["vals"][slot] > 0:
                        self._wait(E, ("d", ring["sems"][slot], (q, slot), ring["vals"][slot]))


class Rot:
    def __init__(self, items):
        self.items = items
        self.i = 0

    def get(self):
        it = self.items[self.i % len(self.items)]
        self.i += 1
        return it


def build_program(n_layers=DEPTH, stop_after=None):
    nc = bass.Bass("TRN2", target_bir_lowering=False)
    dt = nc.dram_tensor
    xT = dt("xT", [KC, 128, S], F32, kind="ExternalInput").ap()
    y = dt("y", [KC, 128, S], F32, kind="ExternalOutput").ap()
    wg_d = [dt(f"wg{i}", [DEPTH, NF, 128, KC, 128], F32, kind="ExternalInput").ap() for i in (1, 2)]
    wu_d = [dt(f"wu{i}", [DEPTH, NF, 128, KC, 128], F32, kind="ExternalInput").ap() for i in (1, 2)]
    wd_d = [dt(f"wd{i}", [DEPTH, NF, 128, D], F32, kind="ExternalInput").ap() for i in (1, 2)]
    win_d = dt("win", [DEPTH, 30, 128, KC, 128], F32, kind="ExternalInput").ap()
    wout_d = dt("wout", [DEPTH, KC, 128, D], F32, kind="ExternalInput").ap()
    bda_d = dt("bda", [DEPTH, NLC, 128, 128], F32, kind="ExternalInput").ap()
    bdx_d = dt("bdx", [DEPTH, NLC, 128, 128], F32, kind="ExternalInput").ap()
    prm_d = dt("prm", [128, NPRM], F32, kind="ExternalInput").ap()
    emask_d = dt("emask", [6, 128, 512], F32, kind="ExternalInput").ap()
    onesbd_d = dt("onesbd", [2, 128, 128], F32, kind="ExternalInput").ap()
    mixs = dt("mixs", [KC, 128, S], BF16, kind="Internal").ap()

    x_view_in = xT.rearrange("k p t -> p k t")
    y_view = y.rearrange("k p t -> p k t")
    mix_view = mixs.rearrange("k p t -> p k t")

    with ExitStack() as stack:
        sc = Sched(nc, stack)
        pe, act, dve, pool, sp = sc.pe, sc.act, sc.dve, sc.pool, sc.sp

        uid = [0]

        def sb(name, shape, dtype, st=None):
            uid[0] += 1
            return (st or stack).enter_context(nc.sbuf_tensor(f"{name}_{uid[0]}", shape, dtype))

        prm = sb("prm_t", [128, NPRM], F32)
        prm_b = Buf("prm")
        ones = sb("ones_t", [128, 2, 128], BF16)
        ones_b = Buf("ones")
        cs = sb("cs_t", [128, DEPTH * NLC], F32)
        cs_b = Buf("cs")
        cstmp = sb("cstmp_t", [128, DEPTH * NLC], F32)
        cstmp_b = Buf("cstmp")
        epsb = sb("epsb_t", [128, 1], F32)
        epsb_b = Buf("epsb")
        sc.op(dve, lambda: nc.vector.memset(epsb[:], EPS), writes=[epsb_b])
        ps_big = stack.enter_context(nc.psum_tensor("ps_big", [128, 4096], F32))
        ps_tiles = [ps_big[:, 512 * i:512 * (i + 1)] for i in range(8)]
        ps_bufs = [Buf(f"ps{i}") for i in range(8)]
        PS = list(zip(ps_tiles, ps_bufs))

        xd_b = [Buf(f"xd{t}") for t in range(NT)]
        mixd_b = [Buf(f"mixd{k}") for k in range(KC)]

        sc.dma("sp", prm[:], prm_d, writes=[prm_b])
        sc.dma("pool", ones[:, 0, :], onesbd_d[0], writes=[ones_b])
        sc.dma("pool", ones[:, 1, :], onesbd_d[1], writes=[ones_b])
        sc.op(act, lambda: nc.scalar.activation(out=cstmp[:], in_=prm[:, P_LM:P_LM + DEPTH * NLC], func=AF.Exp, scale=-1.0),
              reads=[prm_b], writes=[cstmp_b])
        sc.op(act, lambda: nc.scalar.activation(out=cstmp[:], in_=cstmp[:], func=AF.Ln, bias=1.0, scale=1.0),
              reads=[cstmp_b], writes=[cstmp_b])
        sc.op(dve, lambda: nc.vector.tensor_scalar(out=cs[:], in0=cstmp[:], scalar1=-8.0, scalar2=None, op0=ALU.mult),
              reads=[cstmp_b], writes=[cs_b])

        ckt = sb("ck_t", [128, 5, DEPTH * NLC], F32)
        ck_b = Buf("ck")
        sc.op(dve, lambda: nc.vector.tensor_copy(out=ckt[:, 0, :], in_=cs[:]), reads=[cs_b], writes=[ck_b])
        for k_ in range(1, 5):
            sc.op(dve, lambda k_=k_: nc.vector.scalar_tensor_tensor(out=ckt[:, k_, :], in0=ckt[:, k_ - 1, :], scalar=1.0 / (k_ + 1), in1=cs[:],
                                                                    op0=ALU.mult, op1=ALU.mult), reads=[ck_b, cs_b], writes=[ck_b])

        def mm(out, lhsT, rhs, start, stop, reads, writes, inc=None):
            return sc.op(pe, lambda: nc.tensor.matmul(out, lhsT, rhs, start=start, stop=stop),
                         reads=reads, writes=writes, inc=(stop if inc is None else inc))

        def rms_stats(x_t, x_b, sq_rot, psS, sd, sd_b, nfeat_inv, ones_idx, nk, lnexp=False):
            pst, psb = psS
            for k in range(nk):
                sq, sq_b = sq_rot.get()
                sc.op(act, lambda k=k, sq=sq: nc.scalar.activation(out=sq[:], in_=x_t[k], func=AF.Square),
                      reads=[x_b], writes=[sq_b])
                mm(pst[:], ones[:, ones_idx, :], sq[:], k == 0, k == nk - 1, [ones_b, sq_b], [psb], inc=True)
            if lnexp:
                sc.op(act, lambda: nc.scalar.activation(out=sd[:], in_=pst[:], func=AF.Ln, bias=epsb[:, 0:1], scale=nfeat_inv),
                      reads=[psb, epsb_b], writes=[sd_b])
                sc.op(act, lambda: nc.scalar.activation(out=sd[:], in_=sd[:], func=AF.Exp, scale=-0.5),
                      reads=[sd_b], writes=[sd_b])
                return
            sc.op(act, lambda: nc.scalar.activation(out=sd[:], in_=pst[:], func=AF.Sqrt, bias=EPS, scale=nfeat_inv),
                  reads=[psb], writes=[sd_b])
            sc.op(dve, lambda: nc.vector.reciprocal(out=sd[:], in_=sd[:]), reads=[sd_b], writes=[sd_b])

        def ffn_alloc(st):
            return dict(wgu=sb("wgu", [128, NF, 2, KC, 128], BF16, st), wdt=sb("wdt", [128, NF, D], BF16, st),
                        wgu_b=[Buf(f"wgu{f}") for f in range(NF)], wd_b=[Buf(f"wd{f}") for f in range(NF)])

        def ffn_load(W, l, which):
            for f in range(NF):
                sc.dma("pool", W["wgu"][:, f, 0], wg_d[which][l, f], writes=[W["wgu_b"][f]])
                sc.dma("pool", W["wgu"][:, f, 1], wu_d[which][l, f], writes=[W["wgu_b"][f]])
            for f in range(NF):
                sc.dma("pool", W["wdt"][:, f, :], wd_d[which][l, f], writes=[W["wd_b"][f]])

        def ffn_phase(l, which, src_view, first, W=None):
            gcol = (P_G1 if which == 0 else P_G2) + l * KC
            with ExitStack() as st:
                preloaded = W is not None
                if W is None:
                    W = ffn_alloc(st)
                wgu, wdt, wgu_b, wd_b = W["wgu"], W["wdt"], W["wgu_b"], W["wd_b"]
                xts = [sb(f"xt{i}", [128, KC, TT], F32, st) for i in range(2)]
                xt_b = [Buf(f"xt{i}") for i in range(2)]
                ht = sb("ht", [128, KC, TT], BF16, st)
                ht_b = Buf("ht")
                At = sb("At", [128, NF, TT], BF16, st)
                A_b = [Buf(f"A{f}") for f in range(NF)]
                sq_rot = Rot([(sb(f"sq{i}", [128, TT], BF16, st), Buf(f"sq{i}")) for i in range(3)])
                sg_rot = Rot([(sb(f"sg{i}", [128, TT], F32, st), Buf(f"sg{i}")) for i in range(2)])
                sd = sb("sd", [128, TT], F32, st)
                sd_b = Buf("sd")

                if not preloaded:
                    ffn_load(W, l, which)

                G_rot = Rot([PS[0], PS[1]])
                U_rot = Rot([PS[2], PS[3]])
                Y_rot = Rot([PS[4], PS[5]])
                psS = PS[6]

                def load_x(tt):
                    i = tt % 2
                    sc.dma("sp", xts[i][:], src_view[:, :, tt * TT:(tt + 1) * TT],
                           reads=[] if first else [xd_b[tt]], writes=[xt_b[i]])

                def norm_stats(tt):
                    i = tt % 2
                    rms_stats([xts[i][:, k, :] for k in range(KC)], xt_b[i], sq_rot, psS, sd, sd_b, 1.0 / D, 0, KC)

                def norm_scale(tt):
                    i = tt % 2
                    for k in range(KC):
                        sc.op(dve, lambda k=k: nc.vector.scalar_tensor_tensor(
                            out=ht[:, k, :], in0=xts[i][:, k, :], scalar=prm[:, gcol + k:gcol + k + 1], in1=sd[:],
                            op0=ALU.mult, op1=ALU.mult), reads=[xt_b[i], sd_b, prm_b], writes=[ht_b])

                load_x(0)
                norm_stats(0)
                norm_scale(0)
                for tt in range(NT):
                    i = tt % 2
                    if tt + 1 < NT:
                        load_x(tt + 1)
                    for f in range(NF):
                        (pg, pgb), (pu, pub) = G_rot.get(), U_rot.get()
                        for k in range(KC):
                            mm(pg[:], wgu[:, f, 0, k, :], ht[:, k, :], k == 0, k == KC - 1, [wgu_b[f], ht_b], [pgb])
                        for k in range(KC):
                            mm(pu[:], wgu[:, f, 1, k, :], ht[:, k, :], k == 0, k == KC - 1, [wgu_b[f], ht_b], [pub])
                        sg, sg_b = sg_rot.get()
                        sc.op(act, lambda: nc.scalar.activation(out=sg[:], in_=pg[:], func=AF.Silu), reads=[pgb], writes=[sg_b])
                        sc.op(dve, lambda: nc.vector.tensor_tensor(out=At[:, f, :], in0=sg[:], in1=pu[:], op=ALU.mult),
                              reads=[sg_b, pub], writes=[A_b[f]])
                        if f == 12 and tt + 1 < NT:
                            norm_stats(tt + 1)
                    if tt + 1 < NT:
                        norm_scale(tt + 1)
                    for d in range(KC):
                        py, pyb = Y_rot.get()
                        for f in range(NF):
                            mm(py[:], wdt[:, f, d * 128:(d + 1) * 128], At[:, f, :], f == 0, f == NF - 1, [wd_b[f], A_b[f]], [pyb])
                        sc.op(dve, lambda d=d: nc.vector.scalar_tensor_tensor(
                            out=xts[i][:, d, :], in0=py[:], scalar=0.5, in1=xts[i][:, d, :], op0=ALU.mult, op1=ALU.add),
                            reads=[pyb, xt_b[i]], writes=[xt_b[i]])
                    sc.dma("sp", y_view[:, :, tt * TT:(tt + 1) * TT], xts[i][:], reads=[xt_b[i]], writes=[xd_b[tt]])
                sc.barrier()

        def mixer_phase(l, upto=None, m4_hooks=None):
            with ExitStack() as st:
                hT = sb("hT", [128, KC, S], BF16, st)
                hT_b = [Buf(f"hT{t}") for t in range(NT)]
                hT_flat = hT[:].rearrange("p k t -> p (k t)")
                wrot = Rot([(sb(f"wp{i}", [128, KC, 128], BF16, st), Buf(f"wp{i}")) for i in range(6)])
                P_rot = Rot([PS[0], PS[1]])

                with ExitStack() as s1:
                    xts = [sb(f"mx{i}", [128, KC, TT], F32, s1) for i in range(2)]
                    xt_b = [Buf(f"mx{i}") for i in range(2)]
                    sq_rot = Rot([(sb(f"msq{i}", [128, TT], BF16, s1), Buf(f"msq{i}")) for i in range(3)])
                    sd_rot = Rot([(sb(f"msd{i}", [128, TT], F32, s1), Buf(f"msd{i}")) for i in range(2)])
                    gcol = P_GM + l * KC
                    sc.dma("sp", xts[0][:], y_view[:, :, 0:TT], reads=[xd_b[0]], writes=[xt_b[0]])
                    for tt in range(NT):
                        i = tt % 2
                        if tt + 1 < NT:
                            sc.dma("sp", xts[(tt + 1) % 2][:], y_view[:, :, (tt + 1) * TT:(tt + 2) * TT],
                                   reads=[xd_b[tt + 1]], writes=[xt_b[(tt + 1) % 2]])
                        sd, sd_b = sd_rot.get()
                        rms_stats([xts[i][:, k, :] for k in range(KC)], xt_b[i], sq_rot, PS[6 + (tt % 2)], sd, sd_b, 1.0 / D, 0, KC)
                        for k in range(KC):
                            sc.op(dve, lambda k=k: nc.vector.scalar_tensor_tensor(
                                out=hT[:, k, tt * TT:(tt + 1) * TT], in0=xts[i][:, k, :], scalar=prm[:, gcol + k:gcol + k + 1],
                                in1=sd[:], op0=ALU.mult, op1=ALU.mult), reads=[xt_b[i], sd_b, prm_b], writes=[hT_b[tt]])
                    sc.barrier()
                    if upto == "m1":
                        return

                def load_w(blk):
                    w, w_b = wrot.get()
                    sc.dma("pool", w[:], win_d[l, blk], writes=[w_b])
                    return w, w_b

                def proj_tile(w, w_b, tt, ps):
                    pst, psb = ps
                    for k in range(KC):
                        mm(pst[:], w[:, k, :], hT[:, k, tt * TT:(tt + 1) * TT], k == 0, k == KC - 1, [w_b, hT_b[tt]], [psb])

                with ExitStack() as s2:
                    NW = S + 4
                    HS = S // 2
                    LX = [sb(f"llx{i}", [128, NW], F32, s2) for i in range(2)]
                    LX_b = [[Buf(f"llx{i}h{h}") for h in range(2)] for i in range(2)]
                    GL = [sb(f"lgl{i}", [128, S], BF16, s2) for i in range(2)]
                    GL_b = [Buf(f"lgl{i}") for i in range(2)]
                    XC = sb("lxc", [128, S], F32, s2)
                    XC_b = [Buf(f"lxc{h}") for h in range(2)]
                    IG = sb("lig", [128, S], F32, s2)
                    IG_b = [Buf(f"lig{h}") for h in range(2)]
                    PP = sb("lpp", [128, S], F32, s2)
                    PP_b = [Buf(f"lpp{h}") for h in range(2)]
                    xcb = sb("xcb", [128, S], BF16, s2)
                    xcb_b = [Buf(f"xcb{h}") for h in range(2)]
                    ybf = [sb(f"ybf{i}", [128, S], BF16, s2) for i in range(2)]
                    ybf_b = [Buf(f"ybf{i}") for i in range(2)]
                    bd_rot = Rot([(sb(f"bd{i}", [128, 128], BF16, s2), Buf(f"bd{i}")) for i in range(4)])
                    L_rot = Rot(PS)
                    for i in range(2):
                        sc.op(dve, lambda i=i: nc.vector.memset(LX[i][:, 0:3], 0.0), writes=[LX_b[i][0]])

                    def lx_mm(c):
                        wx, wx_b = load_w(18 + c)
                        tiles = []
                        for tt in range(NT):
                            ps = L_rot.get()
                            proj_tile(wx, wx_b, tt, ps)
                            tiles.append(ps)
                        return tiles

                    def lx_evac(c, tiles, tts):
                        for tt in tts:
                            ps = tiles[tt]
                            sc.op(act, lambda: nc.scalar.copy(out=LX[c % 2][:, 3 + tt * TT:3 + (tt + 1) * TT], in_=ps[0][:]),
                                  reads=[ps[1]], writes=[LX_b[c % 2][tt // 4]])

                    def gl_proj(c):
                        wgt, wgt_b = load_w(24 + c)
                        for tt in range(NT):
                            ps = L_rot.get()
                            proj_tile(wgt, wgt_b, tt, ps)
                            sc.op(act, lambda: nc.scalar.activation(out=GL[c % 2][:, tt * TT:(tt + 1) * TT], in_=ps[0][:],
                                                                    func=AF.Gelu_apprx_tanh), reads=[ps[1]], writes=[GL_b[c % 2]])

                    t0_ = lx_mm(0)
                    lx_evac(0, t0_, range(NT))
                    gl_proj(0)
                    for c in range(NLC):
                        pc = l * NLC + c
                        lxt, lxb = LX[c % 2], LX_b[c % 2]
                        bda, bda_b = bd_rot.get()
                        bdx, bdx_b = bd_rot.get()
                        sc.dma("pool", bda[:], bda_d[l, c], writes=[bda_b])
                        sc.dma("pool", bdx[:], bdx_d[l, c], writes=[bdx_b])
                        cw0 = P_CW + pc * 4
                        hc = [slice(h * HS, (h + 1) * HS) for h in range(2)]
                        zd = [lxt[:, 3 + h * HS:3 + (h + 1) * HS] for h in range(2)]
                        for h in range(2):
                            sc.op(act, lambda h=h: nc.scalar.activation(out=XC[:, hc[h]], in_=lxt[:, h * HS:h * HS + HS], func=AF.Identity,
                                                                        scale=prm[:, cw0:cw0 + 1], bias=prm[:, P_CB + pc:P_CB + pc + 1]),
                                  reads=[lxb[0], prm_b] + ([lxb[1]] if h == 1 else []), writes=[XC_b[h]])
                        for h in range(2):
                            lr = [lxb[0]] + ([lxb[1]] if h == 1 else [])
                            for j in range(1, 4):
                                sc.op(dve, lambda j=j, h=h: nc.vector.scalar_tensor_tensor(
                                    out=XC[:, hc[h]], in0=lxt[:, h * HS + j:h * HS + j + HS], scalar=prm[:, cw0 + j:cw0 + j + 1], in1=XC[:, hc[h]],
                                    op0=ALU.mult, op1=ALU.add), reads=lr + [XC_b[h], prm_b], writes=[XC_b[h]])
                        for h in range(2):
                            sc.op(act, lambda h=h: nc.scalar.copy(out=xcb[:, hc[h]], in_=XC[:, hc[h]]), reads=[XC_b[h]], writes=[xcb_b[h]])
                            for tt in range(4 * h, 4 * h + 4):
                                ps = L_rot.get()
                                mm(ps[0][:], bda[:], xcb[:, tt * TT:(tt + 1) * TT], True, True, [bda_b, xcb_b[h]], [ps[1]])
                                sc.op(act, lambda: nc.scalar.activation(out=lxt[:, 3 + tt * TT:3 + (tt + 1) * TT], in_=ps[0][:], func=AF.Sigmoid,
                                                                        bias=prm[:, P_GA + pc:P_GA + pc + 1]),
                                      reads=[ps[1], prm_b], writes=[lxb[h]])
                            for tt in range(4 * h, 4 * h + 4):
                                ps = L_rot.get()
                                mm(ps[0][:], bdx[:], xcb[:, tt * TT:(tt + 1) * TT], True, True, [bdx_b, xcb_b[h]], [ps[1]])
                                sc.op(act, lambda: nc.scalar.activation(out=IG[:, tt * TT:(tt + 1) * TT], in_=ps[0][:], func=AF.Sigmoid,
                                                                        bias=prm[:, P_GX + pc:P_GX + pc + 1]),
                                      reads=[ps[1], prm_b], writes=[IG_b[h]])
                        nxt_tiles = lx_mm(c + 1) if c + 1 < NLC else None
                        for h in range(2):
                            r_ = zd[h]
                            p_ = PP[:, hc[h]]
                            sc.op(dve, lambda: nc.vector.tensor_scalar(out=p_, in0=r_, scalar1=ckt[:, 4, pc:pc + 1], scalar2=ckt[:, 3, pc:pc + 1],
                                                                       op0=ALU.mult, op1=ALU.add), reads=[lxb[h], ck_b], writes=[PP_b[h]])
                            sc.op(dve, lambda: nc.vector.tensor_tensor(out=p_, in0=p_, in1=r_, op=ALU.mult), reads=[PP_b[h], lxb[h]], writes=[PP_b[h]])
                            for k_ in (2, 1, 0):
                                sc.op(dve, lambda k_=k_: nc.vector.scalar_tensor_tensor(out=p_, in0=p_, scalar=ckt[:, k_, pc:pc + 1], in1=r_,
                                                                                        op0=ALU.add, op1=ALU.mult),
                                      reads=[PP_b[h], lxb[h], ck_b], writes=[PP_b[h]])
                            sc.op(act, lambda: nc.scalar.add(out=r_, in_=p_, add=1.0), reads=[PP_b[h]], writes=[lxb[h]])
                            sc.op(dve, lambda: nc.vector.scalar_tensor_tensor(out=p_, in0=p_, scalar=2.0, in1=p_, op0=ALU.add, op1=ALU.mult),
                                  reads=[PP_b[h]], writes=[PP_b[h]])
                            sc.op(act, lambda: nc.scalar.activation(out=p_, in_=p_, func=AF.Sqrt, scale=-1.0), reads=[PP_b[h]], writes=[PP_b[h]])
                            if nxt_tiles is not None:
                                lx_evac(c + 1, nxt_tiles, range(4 * h, 4 * h + 4))
                        if c + 1 < NLC:
                            gl_proj(c + 1)
                        for h in range(2):
                            sc.op(dve, lambda h=h: nc.vector.tensor_tensor(out=IG[:, hc[h]], in0=IG[:, hc[h]], in1=XC[:, hc[h]], op=ALU.mult),
                                  reads=[IG_b[h], XC_b[h]], writes=[IG_b[h]])
                            sc.op(dve, lambda h=h: nc.vector.tensor_tensor(out=IG[:, hc[h]], in0=IG[:, hc[h]], in1=PP[:, hc[h]], op=ALU.mult),
                                  reads=[IG_b[h], PP_b[h]], writes=[IG_b[h]])
                        sc.op(dve, lambda: nc.vector.tensor_tensor_scan(out=XC[:], data0=lxt[:, 3:3 + S], data1=IG[:], initial=0.0,
                                                                        op0=ALU.mult, op1=ALU.add),
                              reads=[lxb[0], lxb[1], IG_b[0], IG_b[1], XC_b[0], XC_b[1]], writes=[XC_b[0], XC_b[1]])
                        yb, yb_b = ybf[c % 2], ybf_b[c % 2]
                        for h in range(2):
                            sc.op(dve, lambda h=h: nc.vector.tensor_tensor(out=yb[:, hc[h]], in0=XC[:, hc[h]], in1=GL[c % 2][:, hc[h]], op=ALU.mult),
                                  reads=[XC_b[h], GL_b[c % 2]], writes=[yb_b])
                        sc.dma("sp", mixs[2 + c], yb[:], reads=[yb_b], writes=[mixd_b[2 + c]])
                        if upto == "lru%d" % c:
                            sc.barrier()
                            return
                    sc.barrier()
                    if upto == "lru":
                        return

                with ExitStack() as s3:
                    acc = [sb(f"acc{i}", [128, S], F32, s3) for i in range(2)]
                    acc_b = [Buf(f"acc{i}") for i in range(2)]
                    qp = [sb(f"qp{s_}", [128, 32, 256], BF16, s3) for s_ in range(2)]
                    qp_b = [Buf(f"qp{s_}") for s_ in range(2)]
                    kTs = [sb(f"kT{s_}", [128, S], BF16, s3) for s_ in range(2)]
                    kT_bs = [Buf(f"kT{s_}") for s_ in range(2)]
                    va = [sb(f"va{s_}", [128, 32, 192], BF16, s3) for s_ in range(2)]
                    va_b = [Buf(f"va{s_}") for s_ in range(2)]
                    em_rot = Rot([(sb(f"em{i}", [128, 512], F32, s3), Buf(f"em{i}")) for i in range(2)])
                    ex_rot = Rot([(sb(f"ex{i}", [128, 512], F32, s3), Buf(f"ex{i}")) for i in range(2)])
                    pt_rot = Rot([(sb(f"pt{i}", [128, 512], BF16, s3), Buf(f"pt{i}")) for i in range(3)])
                    sq_rot = Rot([(sb(f"asq{i}", [128, TT], BF16, s3), Buf(f"asq{i}")) for i in range(2)])
                    sd_rot = Rot([(sb(f"asd{i}", [128, TT], F32, s3), Buf(f"asd{i}")) for i in range(2)])
                    tmp_rot = Rot([(sb(f"atm{i}", [128, TT], F32, s3), Buf(f"atm{i}")) for i in range(1)])
                    att_rot = Rot([(sb(f"att{i}", [128, TT], BF16, s3), Buf(f"att{i}")) for i in range(2)])
                    for s_ in range(2):
                        sc.op(dve, lambda s_=s_: nc.vector.memset(va[s_][:, :, 64:128], 1.0), writes=[va_b[s_]])
                        sc.op(dve, lambda s_=s_: nc.vector.memset(qp[s_][64:128, :, 0:128], 0.0), writes=[qp_b[s_]])
                        sc.op(dve, lambda s_=s_: nc.vector.memset(qp[s_][0:64, :, 128:256], 0.0), writes=[qp_b[s_]])
                    ST_rot = Rot([PS[4], PS[5]])
                    A_rot = Rot([PS[0], PS[1], PS[2]])
                    PSTAT = PS[3]
                    PO_rot = Rot([PS[6], PS[7]])
                    em_of = {}

                    def proj_gen(hp, g, s_):
                        win_, r = GROUPS[g]
                        nb = (S // r) // 128
                        kT, kT_b = kTs[s_], kT_bs[s_]
                        vA, vA_b = va[s_], va_b[s_]
                        emt, emt_b = em_rot.get()
                        sc.dma("sp", emt[:], emask_d[g * 2 + hp], writes=[emt_b])
                        em_of[(hp, g)] = (emt, emt_b)
                        wq_ = load_w(2 * g + hp)
                        wk_ = load_w(6 + 2 * g + hp)
                        items = [(isq, tt) for isq in range(2) for tt in range(NT)]

                        def stepA(isq, tt):
                            w, w_b = (wq_, wk_)[isq]
                            ps = A_rot.get()
                            proj_tile(w, w_b, tt, ps)
                            return ps

                        def stepB(isq, tt, ps):
                            ncol = (P_QN + l, P_KN + l)[isq]
                            sd, sd_b = sd_rot.get()
                            rms_stats([ps[0][:]], ps[1], sq_rot, PSTAT, sd, sd_b, 1.0 / 64.0, 1, 1, lnexp=True)
                            if isq == 1:
                                if r == 1:
                                    o_ap = kT[:, tt * TT:(tt + 1) * TT]
                                    i_ap = ps[0][:]
                                    d_ap = sd[:]
                                else:
                                    na = TT // r
                                    o_ap = kT[:].rearrange("p (c l) -> p c l", c=r)[:, :, tt * na:(tt + 1) * na]
                                    i_ap = ps[0][:].rearrange("p (a c) -> p c a", c=r)
                                    d_ap = sd[:].rearrange("p (a c) -> p c a", c=r)
                                sc.op(dve, lambda: nc.vector.scalar_tensor_tensor(
                                    out=o_ap, in0=i_ap, scalar=prm[:, ncol:ncol + 1], in1=d_ap, op0=ALU.mult, op1=ALU.mult),
                                    reads=[ps[1], sd_b, prm_b], writes=[kT_b])
                            else:
                                for hh in range(2):
                                    rows = slice(hh * 64, hh * 64 + 64)
                                    hoff = hh * 128
                                    if r == 1:
                                        o_ap = qp[s_][rows, 4 * tt:4 * tt + 4, hoff:hoff + 128]
                                        i_ap = ps[0][rows, :].rearrange("p (j i) -> p j i", j=4)
                                        d_ap = sd[rows, :].rearrange("p (j i) -> p j i", j=4)
                                    elif r == 4:
                                        o_ap = qp[s_][rows, :, :].rearrange("p (c t) e -> p c t e", c=4)[:, :, tt, hoff:hoff + 128]
                                        i_ap = ps[0][rows, :].rearrange("p (a c) -> p c a", c=4)
                                        d_ap = sd[rows, :].rearrange("p (a c) -> p c a", c=4)
                                    else:
                                        i0 = hoff + 32 * (tt % 4)
                                        o_ap = qp[s_][rows, :, :].rearrange("p (c t) e -> p c t e", c=16)[:, :, tt // 4, i0:i0 + 32]
                                        i_ap = ps[0][rows, :].rearrange("p (a c) -> p c a", c=16)
                                        d_ap = sd[rows, :].rearrange("p (a c) -> p c a", c=16)
                                    sc.op(dve, lambda: nc.vector.scalar_tensor_tensor(
                                        out=o_ap, in0=i_ap, scalar=prm[rows, ncol:ncol + 1], in1=d_ap, op0=ALU.mult, op1=ALU.mult),
                                        reads=[ps[1], sd_b, prm_b], writes=[qp_b[s_]])

                        nxt_ps = stepA(*items[0])
                        for idx, it_ in enumerate(items):
                            cur_ps = nxt_ps
                            if idx + 1 < len(items):
                                nxt_ps = stepA(*items[idx + 1])
                            stepB(it_[0], it_[1], cur_ps)
                            yield
                        wv, wv_b = load_w(12 + 2 * g + hp)
                        for b4 in range(8):
                            pv, pvb = A_rot.get()
                            for j in range(4):
                                blk = b4 * 4 + j
                                c_, kb = divmod(blk, nb)
                                t0 = c_ + r * 128 * kb
                                tts = list(range(t0 // TT, (t0 + r * 127) // TT + 1))
                                for k in range(KC):
                                    base = k * S + t0
                                    mm(pv[:, j * 128:(j + 1) * 128], hT_flat[:, base:base + 127 * r + 1:r], wv[:, k, :], k == 0, k == KC - 1,
                                       [hT_b[t_] for t_ in tts] + [wv_b], [pvb])
                            pv3 = pv[:].rearrange("p (j e) -> p j e", j=4)
                            sc.op(act, lambda: nc.scalar.copy(out=vA[:, b4 * 4:b4 * 4 + 4, 0:64], in_=pv3[:, :, 0:64]),
                                  reads=[pvb], writes=[vA_b])
                            sc.op(act, lambda: nc.scalar.copy(out=vA[:, b4 * 4:b4 * 4 + 4, 128:192], in_=pv3[:, :, 64:128]),
                                  reads=[pvb], writes=[vA_b])
                            yield

                    LOOK = 1
                    PBURST = 8

                    def attn_gen(hp, g, s_):
                        win_, r = GROUPS[g]
                        nb = (S // r) // 128
                        kT, kT_b = kTs[s_], kT_bs[s_]
                        vA, vA_b = va[s_], va_b[s_]
                        emt, emt_b = em_of[(hp, g)]

                        def qk_block(blk):
                            c_, qb = divmod(blk, nb)
                            cur = slice(blk * 128, (blk + 1) * 128)
                            prv = slice((blk - 1) * 128, blk * 128)
                            stt, stb = ST_rot.get()
                            mm(stt[:, 0:256], kT[:, cur], qp[s_][:, blk, :], True, True, [kT_b, qp_b[s_]], [stb])
                            if qb > 0:
                                mm(stt[:, 256:512], kT[:, prv], qp[s_][:, blk, :], True, True, [kT_b, qp_b[s_]], [stb])
                            return stt, stb

                        pend = [qk_block(b_) for b_ in range(LOOK)]
                        pending_acc = None
                        po, po_b = None, None
                        for blk in range(32):
                            stt, stb = pend.pop(0)
                            if blk + LOOK < 32:
                                pend.append(qk_block(blk + LOOK))
                            c_, qb = divmod(blk, nb)
                            j = blk % 2
                            if j == 0:
                                po, po_b = PO_rot.get()
                            ex, ex_b = ex_rot.get()
                            sc.op(act, lambda: nc.scalar.activation(out=ex[:], in_=stt[:], func=AF.Exp, scale=0.125),
                                  reads=[stb], writes=[ex_b])
                            pt, pt_b = pt_rot.get()
                            sc.op(dve, lambda: nc.vector.tensor_tensor(out=pt[:], in0=ex[:], in1=emt[:], op=ALU.mult),
                                  reads=[ex_b, emt_b], writes=[pt_b])
                            if pending_acc is not None:
                                pending_acc()
                                pending_acc = None
                            for hh in range(2):
                                vc = slice(hh * 64, hh * 64 + 128)
                                oc = slice(hh * 256 + j * 128, hh * 256 + (j + 1) * 128)
                                mm(po[:, oc], vA[:, blk, vc], pt[:, hh * 128:hh * 128 + 128], True, qb == 0, [vA_b, pt_b], [po_b])
                                if qb > 0:
                                    mm(po[:, oc], vA[:, blk - 1, vc], pt[:, 256 + hh * 128:256 + hh * 128 + 128], False, True, [vA_b, pt_b], [po_b])
                            if j == 1:
                                def _acc(blk=blk, po=po, po_b=po_b):
                                    blk0 = blk - 1
                                    c0, qb0 = divmod(blk0, nb)
                                    start = c0 + r * 128 * qb0
                                    for hh in range(2):
                                        a_ap = acc[hh][:, start:start + 255 * r + 1:r]
                                        p_ap = po[:, hh * 256:(hh + 1) * 256]
                                        if g == 0:
                                            sc.op(act, lambda: nc.scalar.copy(out=a_ap, in_=p_ap), reads=[po_b], writes=[acc_b[hh]])
                                        else:
                                            sc.op(dve, lambda: nc.vector.tensor_tensor(out=a_ap, in0=a_ap, in1=p_ap, op=ALU.add),
                                                  reads=[po_b, acc_b[hh]], writes=[acc_b[hh]])
                                pending_acc = _acc
                            yield
                        if pending_acc is not None:
                            pending_acc()

                    def normalize(hp):
                        for tt in range(NT):
                            cols = slice(tt * TT, (tt + 1) * TT)
                            tm, tm_b = tmp_rot.get()
                            at, at_b = att_rot.get()
                            sc.op(dve, lambda: nc.vector.tensor_copy(out=tm[0:64, :], in_=acc[0][64:128, cols]), reads=[acc_b[0]], writes=[tm_b])
                            sc.op(dve, lambda: nc.vector.tensor_copy(out=tm[64:128, :], in_=acc[1][0:64, cols]), reads=[acc_b[1]], writes=[tm_b])
                            sc.op(act, lambda: nc.scalar.activation(out=tm[:], in_=tm[:], func=AF.Ln), reads=[tm_b], writes=[tm_b])
                            sc.op(act, lambda: nc.scalar.activation(out=tm[:], in_=tm[:], func=AF.Exp, scale=-1.0), reads=[tm_b], writes=[tm_b])
                            sc.op(dve, lambda: nc.vector.tensor_tensor(out=at[0:64, :], in0=acc[0][0:64, cols], in1=tm[0:64, :], op=ALU.mult),
                                  reads=[acc_b[0], tm_b], writes=[at_b])
                            sc.op(dve, lambda: nc.vector.tensor_tensor(out=at[64:128, :], in0=acc[1][64:128, cols], in1=tm[64:128, :], op=ALU.mult),
                                  reads=[acc_b[1], tm_b], writes=[at_b])
                            sc.dma("sp", mixs[hp][:, cols], at[:], reads=[at_b], writes=[mixd_b[hp]])

                    seq = [(hp, g) for hp in range(2) for g in range(3)]
                    for _ in proj_gen(seq[0][0], seq[0][1], 0):
                        pass
                    for i_, (hp, g) in enumerate(seq):
                        A = attn_gen(hp, g, i_ % 2)
                        P = proj_gen(seq[i_ + 1][0], seq[i_ + 1][1], (i_ + 1) % 2) if i_ + 1 < len(seq) else None
                        n_p = 24
                        done_p = 0
                        for step, _ in enumerate(A):
                            if P is not None:
                                want = ((((step + 1) * n_p) // 32) // PBURST) * PBURST
                                while done_p < want:
                                    next(P, None)
                                    done_p += 1
                        if P is not None:
                            for _ in P:
                                pass
                        if g == 2:
                            normalize(hp)
                    sc.barrier()
                    if upto == "attn":
                        return
            if m4_hooks is not None:
                m4_hooks[0]()
            with ExitStack() as s4:
                wo = sb("wo", [128, KC, D], BF16, s4)
                wo_b = [Buf(f"wo{k}") for k in range(KC)]
                xts = [sb(f"ox{i}", [128, KC, TT], F32, s4) for i in range(2)]
                xt_b = [Buf(f"ox{i}") for i in range(2)]
                mts = [sb(f"om{i}", [128, KC, TT], BF16, s4) for i in range(2)]
                mt_b = [Buf(f"om{i}") for i in range(2)]
                Y_rot = Rot([PS[0], PS[1], PS[2], PS[3]])
                for k in range(KC):
                    sc.dma("pool", wo[:, k, :], wout_d[l, k], writes=[wo_b[k]])
                if m4_hooks is not None:
                    m4_hooks[1]()

                def load(tt):
                    i = tt % 2
                    sc.dma("sp", xts[i][:], y_view[:, :, tt * TT:(tt + 1) * TT], reads=[xd_b[tt]], writes=[xt_b[i]])
                    sc.dma("sp", mts[i][:], mix_view[:, :, tt * TT:(tt + 1) * TT], reads=mixd_b, writes=[mt_b[i]])

                load(0)
                for tt in range(NT):
                    i = tt % 2
                    if tt + 1 < NT:
                        load(tt + 1)
                    for d in range(KC):
                        py, pyb = Y_rot.get()
                        for k in range(KC):
                            mm(py[:], wo[:, k, d * 128:(d + 1) * 128], mts[i][:, k, :], k == 0, k == KC - 1, [wo_b[k], mt_b[i]], [pyb])
                        sc.op(dve, lambda d=d: nc.vector.tensor_tensor(out=xts[i][:, d, :], in0=py[:], in1=xts[i][:, d, :], op=ALU.add),
                              reads=[pyb, xt_b[i]], writes=[xt_b[i]])
                    sc.dma("sp", y_view[:, :, tt * TT:(tt + 1) * TT], xts[i][:], reads=[xt_b[i]], writes=[xd_b[tt]])
                sc.barrier()

        done = False
        for l in range(n_layers):
            ffn_phase(l, 0, x_view_in if l == 0 else y_view, first=(l == 0))
            if stop_after == (l, "ffn1"):
                done = True
                break
            if stop_after is not None and stop_after[0] == l and len(stop_after) == 3:
                mixer_phase(l, stop_after[2])
                break
            if stop_after == (l, "mixer"):
                mixer_phase(l)
                done = True
                break
            with ExitStack() as wst:
                hold = {}

                def _alloc():
                    hold["W"] = ffn_alloc(wst)

                def _load(l=l):
                    ffn_load(hold["W"], l, 1)

                if PREFETCH_FFN2:
                    mixer_phase(l, m4_hooks=(_alloc, _load))
                    ffn_phase(l, 1, y_view, first=False, W=hold["W"])
                else:
                    mixer_phase(l)
                    ffn_phase(l, 1, y_view, first=False)
        sc.barrier()
        build_program.stats = dict(cnt={n: e.cnt for n, e in sc.engs.items()}, waits=sc.n_wait)
    return nc


def _slopes():
    h = np.arange(1, 13, dtype=np.float64)
    return np.exp2(-8.0 * h / 12.0).reshape(3, 4)


def _emask():
    sl = _slopes()
    i = np.arange(128)[:, None].astype(np.float64)
    j = np.arange(128)[None, :].astype(np.float64)
    out = np.zeros((6, 128, 512), np.float32)
    for g, (_, r) in enumerate(GROUPS):
        for hp in range(2):
            for hh in range(2):
                a = sl[g, 2 * hp + hh] * r
                cur = np.where(j >= i, np.exp(-a * (j - i)), 0.0)
                prv = np.where(i >= j, np.exp(-a * (128.0 + j - i)), 0.0)
                out[g * 2 + hp, :, hh * 128:hh * 128 + 128] = cur
                out[g * 2 + hp, :, 256 + hh * 128:256 + hh * 128 + 128] = prv
    return out


def _prep_weights(inp):
    f32 = np.float32
    c = np.ascontiguousarray
    out = {}
    for i, nm in ((1, "ffn1"), (2, "ffn2")):
        for key, src in ((f"wg{i}", f"{nm}_w_gate"), (f"wu{i}", f"{nm}_w_up")):
            w = np.asarray(inp[src], f32).reshape(DEPTH, KC, 128, NF, 128)
            out[key] = c(w.transpose(0, 3, 2, 1, 4))
        out[f"wd{i}"] = c(np.asarray(inp[f"{nm}_w_down"], f32).reshape(DEPTH, NF, 128, D))
    w = np.asarray(inp["w_in"], f32).reshape(DEPTH, KC, 128, 30, 128)
    out["win"] = c(w.transpose(0, 3, 2, 1, 4))
    out["wout"] = c(np.asarray(inp["w_out"], f32).reshape(DEPTH, KC, 128, D))
    for key, src in (("bda", "gate_a_w"), ("bdx", "gate_x_w")):
        gw = np.asarray(inp[src], f32)
        bd = np.zeros((DEPTH, NLC, 128, 128), f32)
        for cidx in range(NLC):
            bd[:, cidx, 0:64, 0:64] = gw[:, 2 * cidx]
            bd[:, cidx, 64:128, 64:128] = gw[:, 2 * cidx + 1]
        out[key] = bd
    prm = np.zeros((128, NPRM), f32)
    for base, src in ((P_G1, "ffn1_norm"), (P_GM, "mix_norm"), (P_G2, "ffn2_norm")):
        g_ = np.asarray(inp[src], f32).reshape(DEPTH, KC, 128)
        prm[:, base:base + DEPTH * KC] = g_.transpose(2, 0, 1).reshape(128, DEPTH * KC)
    prm[:, P_QN:P_QN + DEPTH] = np.tile(np.asarray(inp["q_norm"], f32).T, (2, 1))
    prm[:, P_KN:P_KN + DEPTH] = np.tile(np.asarray(inp["k_norm"], f32).T, (2, 1))
    cw = np.asarray(inp["conv_w"], f32).reshape(DEPTH, 4, NLC, 128)
    prm[:, P_CW:P_CW + DEPTH * NLC * 4] = cw.transpose(3, 0, 2, 1).reshape(128, DEPTH * NLC * 4)
    for base, src in ((P_CB, "conv_b"), (P_GA, "gate_a_b"), (P_GX, "gate_x_b"), (P_LM, "lru_lambda")):
        v = np.asarray(inp[src], f32).reshape(DEPTH, NLC, 128)
        prm[:, base:base + DEPTH * NLC] = v.transpose(2, 0, 1).reshape(128, DEPTH * NLC)
    out["prm"] = prm
    out["emask"] = _emask()
    ob = np.zeros((2, 128, 128), f32)
    ob[0] = 1.0
    ob[1, 0:64, 0:64] = 1.0
    ob[1, 64:128, 64:128] = 1.0
    out["onesbd"] = ob
    return out


def _run(inp, cores, n_layers=DEPTH, stop_after=None, trace=False):
    shared = _prep_weights(inp)
    x = np.asarray(inp["x"], np.float32)
    nc = build_program(n_layers, stop_after)
    in_maps = []
    for b in cores:
        m = dict(shared)
        m["xT"] = np.ascontiguousarray(x[b].T).reshape(KC, 128, S)
        in_maps.append(m)
    res = run_bass_kernel_spmd(nc, in_maps, core_ids=list(range(len(cores))), trace=trace)
    outs = [np.ascontiguousarray(r["y"].reshape(D, S).T) for r in res.results]
    return outs, res


def kernel(**inputs):
    outs, _ = _run(inputs, list(range(NCORES)))
    return np.stack(outs, axis=0).astype(np.float32)
```

```python
import math
from contextlib import ExitStack

import numpy as np
import concourse.bass as bass
import concourse.mybir as mybir
from concourse.bass_utils import run_bass_kernel_spmd

F32 = mybir.dt.float32
BF16 = mybir.dt.bfloat16
AF = mybir.ActivationFunctionType
ALU = mybir.AluOpType

D = 1024
S = 4096
DEPTH = 4
DFF = 2816
NF = DFF // 128
KC = D // 128
TT = 512
NT = S // TT
DLRU = 768
NLC = DLRU // 128
GROUPS = ((128, 1), (512, 4), (2048, 16))
EPS = 1e-6
NCORES = 8
PREFETCH_FFN2 = False

P_G1 = 0
P_GM = P_G1 + DEPTH * KC
P_G2 = P_GM + DEPTH * KC
P_QN = P_G2 + DEPTH * KC
P_KN = P_QN + DEPTH
P_CW = P_KN + DEPTH
P_CB = P_CW + DEPTH * NLC * 4
P_GA = P_CB + DEPTH * NLC
P_GX = P_GA + DEPTH * NLC
P_LM = P_GX + DEPTH * NLC
NPRM = P_LM + DEPTH * NLC


class Eng:
    def __init__(self, name, h, sem):
        self.name = name
        self.h = h
        self.sem = sem
        self.cnt = 0
        self.pending = False
        self.known = {}


class Buf:
    __slots__ = ("name", "w", "wd", "r", "rd")

    def __init__(self, name):
        self.name = name
        self.w = {}
        self.wd = []
        self.r = {}
        self.rd = []


class Sched:
    KRING = 8

    def __init__(self, nc, stack):
        self.nc = nc

        def mk(n, h):
            return Eng(n, h, stack.enter_context(nc.semaphore("sem_" + n)))

        self.pe = mk("pe", nc.tensor)
        self.act = mk("act", nc.scalar)
        self.dve = mk("dve", nc.vector)
        self.pool = mk("pool", nc.gpsimd)
        self.sp = mk("sp", nc.sync)
        self.engs = {e.name: e for e in (self.pe, self.act, self.dve, self.pool, self.sp)}
        self.rings = {}
        for q in ("sp", "pool"):
            sems = [stack.enter_context(nc.semaphore(f"dma_{q}{i}")) for i in range(self.KRING)]
            self.rings[q] = {"sems": sems, "vals": [0] * self.KRING, "i": 0}
        self.n_wait = 0

    def _wait(self, E, tok):
        if tok[0] == "e":
            _, n, v = tok
            if E.known.get(n, 0) >= v:
                return
            E.h.wait_ge(self.engs[n].sem, v)
            E.known[n] = v
        else:
            _, sem, key, v = tok
            if E.known.get(key, 0) >= v:
                return
            E.h.wait_ge(sem, v)
            E.known[key] = v
        self.n_wait += 1

    def _deps(self, E, reads, writes):
        toks = []
        for b in reads:
            for n, v in b.w.items():
                if n == E.name and n == "pe":
                    continue
                toks.append(("e", n, v))
            toks.extend(b.wd)
        for b in writes:
            if b.r or b.rd:
                for n, v in b.r.items():
                    if n == E.name and n == "pe":
                        continue
                    toks.append(("e", n, v))
                toks.extend(b.rd)
        return toks

    def _commit(self, E, reads, writes, val=None, tok=None):
        for b in writes:
            if b.r or b.rd:
                b.w = {}
                b.wd = []
                b.r = {}
                b.rd = []
            if tok is None:
                b.w[E.name] = val
            else:
                b.wd.append(tok)
        for b in reads:
            if tok is None:
                b.r[E.name] = val
            else:
                b.rd.append(tok)

    def op(self, E, emit, reads=(), writes=(), inc=True):
        for t in self._deps(E, reads, writes):
            self._wait(E, t)
        ins = emit()
        if inc:
            E.cnt += 1
            ins.then_inc(E.sem, 1)
            val = E.cnt
            E.pending = False
        else:
            val = E.cnt + 1
            E.pending = True
        self._commit(E, reads, writes, val=val)
        return ins

    def dma(self, q, out, in_, reads=(), writes=()):
        E = self.engs[q]
        ring = self.rings[q]
        for t in self._deps(E, reads, writes):
            self._wait(E, t)
        slot = ring["i"] % self.KRING
        ring["i"] += 1
        sem = ring["sems"][slot]
        key = (q, slot)
        if ring["vals"][slot] > 0:
            self._wait(E, ("d", sem, key, ring["vals"][slot]))
        E.h.dma_start(out=out, in_=in_).then_inc(sem, 16)
        ring["vals"][slot] += 16
        tok = ("d", sem, key, ring["vals"][slot])
        self._commit(E, reads, writes, tok=tok)

    def barrier(self):
        assert not self.pe.pending
        for E in self.engs.values():
            for F in self.engs.values():
                if F.cnt > 0 and not (F is E and E.name in ("pe", "sp")):
                    self._wait(E, ("e", F.name, F.cnt))
            for q, ring in self.rings.items():
                for slot in range(self.KRING):
                    if ring["vals"][slot] > 0:
                        self._wait(E, ("d", ring["sems"][slot], (q, slot), ring["vals"][slot]))


class Rot:
    def __init__(self, items):
        self.items = items
        self.i = 0

    def get(self):
        it = self.items[self.i % len(self.items)]
        self.i += 1
        return it


def build_program(n_layers=DEPTH, stop_after=None):
    nc = bass.Bass("TRN2", target_bir_lowering=False)
    dt = nc.dram_tensor
    xT = dt("xT", [KC, 128, S], F32, kind="ExternalInput").ap()
    y = dt("y", [KC, 128, S], F32, kind="ExternalOutput").ap()
    wg_d = [dt(f"wg{i}", [DEPTH, NF, 128, KC, 128], F32, kind="ExternalInput").ap() for i in (1, 2)]
    wu_d = [dt(f"wu{i}", [DEPTH, NF, 128, KC, 128], F32, kind="ExternalInput").ap() for i in (1, 2)]
    wd_d = [dt(f"wd{i}", [DEPTH, NF, 128, D], F32, kind="ExternalInput").ap() for i in (1, 2)]
    win_d = dt("win", [DEPTH, 30, 128, KC, 128], F32, kind="ExternalInput").ap()
    wout_d = dt("wout", [DEPTH, KC, 128, D], F32, kind="ExternalInput").ap()
    bda_d = dt("bda", [DEPTH, NLC, 128, 128], F32, kind="ExternalInput").ap()
    bdx_d = dt("bdx", [DEPTH, NLC, 128, 128], F32, kind="ExternalInput").ap()
    prm_d = dt("prm", [128, NPRM], F32, kind="ExternalInput").ap()
    emask_d = dt("emask", [6, 128, 512], F32, kind="ExternalInput").ap()
    onesbd_d = dt("onesbd", [2, 128, 128], F32, kind="ExternalInput").ap()
    mixs = dt("mixs", [KC, 128, S], BF16, kind="Internal").ap()

    x_view_in = xT.rearrange("k p t -> p k t")
    y_view = y.rearrange("k p t -> p k t")
    mix_view = mixs.rearrange("k p t -> p k t")

    with ExitStack() as stack:
        sc = Sched(nc, stack)
        pe, act, dve, pool, sp = sc.pe, sc.act, sc.dve, sc.pool, sc.sp

        uid = [0]

        def sb(name, shape, dtype, st=None):
            uid[0] += 1
            return (st or stack).enter_context(nc.sbuf_tensor(f"{name}_{uid[0]}", shape, dtype))

        prm = sb("prm_t", [128, NPRM], F32)
        prm_b = Buf("prm")
        ones = sb("ones_t", [128, 2, 128], BF16)
        ones_b = Buf("ones")
        cs = sb("cs_t", [128, DEPTH * NLC], F32)
        cs_b = Buf("cs")
        cstmp = sb("cstmp_t", [128, DEPTH * NLC], F32)
        cstmp_b = Buf("cstmp")
        epsb = sb("epsb_t", [128, 1], F32)
        epsb_b = Buf("epsb")
        sc.op(dve, lambda: nc.vector.memset(epsb[:], EPS), writes=[epsb_b])
        ps_big = stack.enter_context(nc.psum_tensor("ps_big", [128, 4096], F32))
        ps_tiles = [ps_big[:, 512 * i:512 * (i + 1)] for i in range(8)]
        ps_bufs = [Buf(f"ps{i}") for i in range(8)]
        PS = list(zip(ps_tiles, ps_bufs))

        xd_b = [Buf(f"xd{t}") for t in range(NT)]
        mixd_b = [Buf(f"mixd{k}") for k in range(KC)]

        sc.dma("sp", prm[:], prm_d, writes=[prm_b])
        sc.dma("pool", ones[:, 0, :], onesbd_d[0], writes=[ones_b])
        sc.dma("pool", ones[:, 1, :], onesbd_d[1], writes=[ones_b])
        sc.op(act, lambda: nc.scalar.activation(out=cstmp[:], in_=prm[:, P_LM:P_LM + DEPTH * NLC], func=AF.Exp, scale=-1.0),
              reads=[prm_b], writes=[cstmp_b])
        sc.op(act, lambda: nc.scalar.activation(out=cstmp[:], in_=cstmp[:], func=AF.Ln, bias=1.0, scale=1.0),
              reads=[cstmp_b], writes=[cstmp_b])
        sc.op(dve, lambda: nc.vector.tensor_scalar(out=cs[:], in0=cstmp[:], scalar1=-8.0, scalar2=None, op0=ALU.mult),
              reads=[cstmp_b], writes=[cs_b])

        ckt = sb("ck_t", [128, 5, DEPTH * NLC], F32)
        ck_b = Buf("ck")
        sc.op(dve, lambda: nc.vector.tensor_copy(out=ckt[:, 0, :], in_=cs[:]), reads=[cs_b], writes=[ck_b])
        for k_ in range(1, 5):
            sc.op(dve, lambda k_=k_: nc.vector.scalar_tensor_tensor(out=ckt[:, k_, :], in0=ckt[:, k_ - 1, :], scalar=1.0 / (k_ + 1), in1=cs[:],
                                                                    op0=ALU.mult, op1=ALU.mult), reads=[ck_b, cs_b], writes=[ck_b])

        def mm(out, lhsT, rhs, start, stop, reads, writes, inc=None):
            return sc.op(pe, lambda: nc.tensor.matmul(out, lhsT, rhs, start=start, stop=stop),
                         reads=reads, writes=writes, inc=(stop if inc is None else inc))

        def rms_stats(x_t, x_b, sq_rot, psS, sd, sd_b, nfeat_inv, ones_idx, nk, lnexp=False):
            pst, psb = psS
            for k in range(nk):
                sq, sq_b = sq_rot.get()
                sc.op(act, lambda k=k, sq=sq: nc.scalar.activation(out=sq[:], in_=x_t[k], func=AF.Square),
                      reads=(list(x_b) if isinstance(x_b, (list, tuple)) else [x_b]), writes=[sq_b])
                mm(pst[:], ones[:, ones_idx, :], sq[:], k == 0, k == nk - 1, [ones_b, sq_b], [psb], inc=True)
            if lnexp:
                sc.op(act, lambda: nc.scalar.activation(out=sd[:], in_=pst[:], func=AF.Ln, bias=epsb[:, 0:1], scale=nfeat_inv),
                      reads=[psb, epsb_b], writes=[sd_b])
                sc.op(act, lambda: nc.scalar.activation(out=sd[:], in_=sd[:], func=AF.Exp, scale=-0.5),
                      reads=[sd_b], writes=[sd_b])
                return
            sc.op(act, lambda: nc.scalar.activation(out=sd[:], in_=pst[:], func=AF.Sqrt, bias=EPS, scale=nfeat_inv),
                  reads=[psb], writes=[sd_b])
            sc.op(dve, lambda: nc.vector.reciprocal(out=sd[:], in_=sd[:]), reads=[sd_b], writes=[sd_b])

        def ffn_alloc(st):
            return dict(wgu=sb("wgu", [128, NF, 2, KC, 128], BF16, st), wdt=sb("wdt", [128, NF, D], BF16, st),
                        wgu_b=[Buf(f"wgu{f}") for f in range(NF)], wd_b=[Buf(f"wd{f}") for f in range(NF)])

        def ffn_load(W, l, which):
            for f in range(NF):
                sc.dma("pool", W["wgu"][:, f, 0], wg_d[which][l, f], writes=[W["wgu_b"][f]])
                sc.dma("pool", W["wgu"][:, f, 1], wu_d[which][l, f], writes=[W["wgu_b"][f]])
            for f in range(NF):
                sc.dma("pool", W["wdt"][:, f, :], wd_d[which][l, f], writes=[W["wd_b"][f]])

        def ffn_phase(l, which, src_view, first, W=None):
            gcol = (P_G1 if which == 0 else P_G2) + l * KC
            with ExitStack() as st:
                preloaded = W is not None
                if W is None:
                    W = ffn_alloc(st)
                wgu, wdt, wgu_b, wd_b = W["wgu"], W["wdt"], W["wgu_b"], W["wd_b"]
                xts = [sb(f"xt{i}", [128, KC, TT], F32, st) for i in range(2)]
                xt_b = [Buf(f"xt{i}") for i in range(2)]
                ht = sb("ht", [128, KC, TT], BF16, st)
                ht_b = Buf("ht")
                At = sb("At", [128, NF, TT], BF16, st)
                A_b = [Buf(f"A{f}") for f in range(NF)]
                sq_rot = Rot([(sb(f"sq{i}", [128, TT], BF16, st), Buf(f"sq{i}")) for i in range(3)])
                sg_rot = Rot([(sb(f"sg{i}", [128, TT], F32, st), Buf(f"sg{i}")) for i in range(2)])
                sd = sb("sd", [128, TT], F32, st)
                sd_b = Buf("sd")

                if not preloaded:
                    ffn_load(W, l, which)

                G_rot = Rot([PS[0], PS[1]])
                U_rot = Rot([PS[2], PS[3]])
                Y_rot = Rot([PS[4], PS[5]])
                psS = PS[6]

                def load_x(tt):
                    i = tt % 2
                    sc.dma("sp", xts[i][:], src_view[:, :, tt * TT:(tt + 1) * TT],
                           reads=[] if first else [xd_b[tt]], writes=[xt_b[i]])

                def norm_stats(tt):
                    i = tt % 2
                    rms_stats([xts[i][:, k, :] for k in range(KC)], xt_b[i], sq_rot, psS, sd, sd_b, 1.0 / D, 0, KC)

                def norm_scale(tt):
                    i = tt % 2
                    for k in range(KC):
                        sc.op(dve, lambda k=k: nc.vector.scalar_tensor_tensor(
                            out=ht[:, k, :], in0=xts[i][:, k, :], scalar=prm[:, gcol + k:gcol + k + 1], in1=sd[:],
                            op0=ALU.mult, op1=ALU.mult), reads=[xt_b[i], sd_b, prm_b], writes=[ht_b])

                load_x(0)
                norm_stats(0)
                norm_scale(0)
                for tt in range(NT):
                    i = tt % 2
                    if tt + 1 < NT:
                        load_x(tt + 1)
                    for f in range(NF):
                        (pg, pgb), (pu, pub) = G_rot.get(), U_rot.get()
                        for k in range(KC):
                            mm(pg[:], wgu[:, f, 0, k, :], ht[:, k, :], k == 0, k == KC - 1, [wgu_b[f], ht_b], [pgb])
                        for k in range(KC):
                            mm(pu[:], wgu[:, f, 1, k, :], ht[:, k, :], k == 0, k == KC - 1, [wgu_b[f], ht_b], [pub])
                        sg, sg_b = sg_rot.get()
                        sc.op(act, lambda: nc.scalar.activation(out=sg[:], in_=pg[:], func=AF.Silu), reads=[pgb], writes=[sg_b])
                        sc.op(dve, lambda: nc.vector.tensor_tensor(out=At[:, f, :], in0=sg[:], in1=pu[:], op=ALU.mult),
                              reads=[sg_b, pub], writes=[A_b[f]])
                        if f == 12 and tt + 1 < NT:
                            norm_stats(tt + 1)
                    if tt + 1 < NT:
                        norm_scale(tt + 1)
                    for d in range(KC):
                        py, pyb = Y_rot.get()
                        for f in range(NF):
                            mm(py[:], wdt[:, f, d * 128:(d + 1) * 128], At[:, f, :], f == 0, f == NF - 1, [wd_b[f], A_b[f]], [pyb])
                        sc.op(dve, lambda d=d: nc.vector.scalar_tensor_tensor(
                            out=xts[i][:, d, :], in0=py[:], scalar=0.5, in1=xts[i][:, d, :], op0=ALU.mult, op1=ALU.add),
                            reads=[pyb, xt_b[i]], writes=[xt_b[i]])
                    sc.dma("sp", y_view[:, :, tt * TT:(tt + 1) * TT], xts[i][:], reads=[xt_b[i]], writes=[xd_b[tt]])
                sc.barrier()

        def mixer_phase(l, upto=None, m4_hooks=None):
            with ExitStack() as st:
                hT = sb("hT", [128, KC, S], BF16, st)
                hT_b = [Buf(f"hT{t}") for t in range(NT)]
                hT_flat = hT[:].rearrange("p k t -> p (k t)")
                wrot = Rot([(sb(f"wp{i}", [128, KC, 128], BF16, st), Buf(f"wp{i}")) for i in range(6)])
                P_rot = Rot([PS[0], PS[1]])

                def load_w(blk):
                    w, w_b = wrot.get()
                    sc.dma("pool", w[:], win_d[l, blk], writes=[w_b])
                    return w, w_b

                def proj_tile(w, w_b, tt, ps):
                    pst, psb = ps
                    for k in range(KC):
                        mm(pst[:], w[:, k, :], hT[:, k, tt * TT:(tt + 1) * TT], k == 0, k == KC - 1, [w_b, hT_b[tt]], [psb])

                with ExitStack() as s2:
                    NW = S + 4
                    HS = S // 2
                    LX = [sb(f"llx{i}", [128, NW], F32, s2) for i in range(2)]
                    LX_b = [[Buf(f"llx{i}h{h}") for h in range(2)] for i in range(2)]
                    GL = [sb(f"lgl{i}", [128, S], BF16, s2) for i in range(2)]
                    GL_b = [Buf(f"lgl{i}") for i in range(2)]
                    XC = sb("lxc", [128, S], F32, s2)
                    XC_b = [Buf(f"lxc{h}") for h in range(2)]
                    IG = sb("lig", [128, S], F32, s2)
                    IG_b = [Buf(f"lig{h}") for h in range(2)]
                    PP = sb("lpp", [128, S], F32, s2)
                    PP_b = [Buf(f"lpp{h}") for h in range(2)]
                    xcb = sb("xcb", [128, S], BF16, s2)
                    xcb_b = [Buf(f"xcb{h}") for h in range(2)]
                    ybf = [sb(f"ybf{i}", [128, S], BF16, s2) for i in range(2)]
                    ybf_b = [Buf(f"ybf{i}") for i in range(2)]
                    bd_rot = Rot([(sb(f"bd{i}", [128, 128], BF16, s2), Buf(f"bd{i}")) for i in range(4)])
                    L_rot = Rot(PS)
                    for i in range(2):
                        sc.op(dve, lambda i=i: nc.vector.memset(LX[i][:, 0:3], 0.0), writes=[LX_b[i][0]])

                    def lx_mm(c):
                        wx, wx_b = load_w(18 + c)
                        tiles = []
                        for tt in range(NT):
                            ps = L_rot.get()
                            proj_tile(wx, wx_b, tt, ps)
                            tiles.append(ps)
                        return tiles

                    def lx_evac(c, tiles, tts):
                        for tt in tts:
                            ps = tiles[tt]
                            sc.op(act, lambda: nc.scalar.copy(out=LX[c % 2][:, 3 + tt * TT:3 + (tt + 1) * TT], in_=ps[0][:]),
                                  reads=[ps[1]], writes=[LX_b[c % 2][tt // 4]])

                    def gl_proj(c):
                        wgt, wgt_b = load_w(24 + c)
                        for tt in range(NT):
                            ps = L_rot.get()
                            proj_tile(wgt, wgt_b, tt, ps)
                            sc.op(act, lambda: nc.scalar.activation(out=GL[c % 2][:, tt * TT:(tt + 1) * TT], in_=ps[0][:],
                                                                    func=AF.Gelu_apprx_tanh), reads=[ps[1]], writes=[GL_b[c % 2]])

                    xts_m = [XC[:].rearrange("p (k t) -> p k t", k=KC), IG[:].rearrange("p (k t) -> p k t", k=KC)]
                    xtw = [[XC_b[0], XC_b[1]], [IG_b[0], IG_b[1]]]
                    sq_rot_m = Rot([(GL[0][:, 0:TT], GL_b[0]), (GL[1][:, 0:TT], GL_b[1]), (xcb[:, 0:TT], xcb_b[0])])
                    sd_rot_m = Rot([(PP[:, 0:TT], PP_b[0]), (PP[:, HS:HS + TT], PP_b[1])])
                    gcol = P_GM + l * KC
                    sc.dma("sp", xts_m[0], y_view[:, :, 0:TT], reads=[xd_b[0]], writes=xtw[0])
                    for tt in range(NT):
                        i = tt % 2
                        if tt + 1 < NT:
                            sc.dma("sp", xts_m[(tt + 1) % 2], y_view[:, :, (tt + 1) * TT:(tt + 2) * TT],
                                   reads=[xd_b[tt + 1]], writes=xtw[(tt + 1) % 2])
                        sd, sd_b = sd_rot_m.get()
                        rms_stats([xts_m[i][:, k, :] for k in range(KC)], xtw[i], sq_rot_m, PS[6 + (tt % 2)], sd, sd_b, 1.0 / D, 0, KC)
                        for k in range(KC):
                            sc.op(dve, lambda k=k: nc.vector.scalar_tensor_tensor(
                                out=hT[:, k, tt * TT:(tt + 1) * TT], in0=xts_m[i][:, k, :], scalar=prm[:, gcol + k:gcol + k + 1],
                                in1=sd[:], op0=ALU.mult, op1=ALU.mult), reads=xtw[i] + [sd_b, prm_b], writes=[hT_b[tt]])
                    if upto == "m1":
                        sc.barrier()
                        return

                    t0_ = lx_mm(0)
                    lx_evac(0, t0_, range(NT))
                    gl_proj(0)
                    for c in range(NLC):
                        pc = l * NLC + c
                        lxt, lxb = LX[c % 2], LX_b[c % 2]
                        bda, bda_b = bd_rot.get()
                        bdx, bdx_b = bd_rot.get()
                        sc.dma("pool", bda[:], bda_d[l, c], writes=[bda_b])
                        sc.dma("pool", bdx[:], bdx_d[l, c], writes=[bdx_b])
                        cw0 = P_CW + pc * 4
                        hc = [slice(h * HS, (h + 1) * HS) for h in range(2)]
                        zd = [lxt[:, 3 + h * HS:3 + (h + 1) * HS] for h in range(2)]
                        for h in range(2):
                            sc.op(act, lambda h=h: nc.scalar.activation(out=XC[:, hc[h]], in_=lxt[:, h * HS:h * HS + HS], func=AF.Identity,
                                                                        scale=prm[:, cw0:cw0 + 1], bias=prm[:, P_CB + pc:P_CB + pc + 1]),
                                  reads=[lxb[0], prm_b] + ([lxb[1]] if h == 1 else []), writes=[XC_b[h]])
                        for h in range(2):
                            lr = [lxb[0]] + ([lxb[1]] if h == 1 else [])
                            for j in range(1, 4):
                                sc.op(dve, lambda j=j, h=h: nc.vector.scalar_tensor_tensor(
                                    out=XC[:, hc[h]], in0=lxt[:, h * HS + j:h * HS + j + HS], scalar=prm[:, cw0 + j:cw0 + j + 1], in1=XC[:, hc[h]],
                                    op0=ALU.mult, op1=ALU.add), reads=lr + [XC_b[h], prm_b], writes=[XC_b[h]])
                        for h in range(2):
                            sc.op(act, lambda h=h: nc.scalar.copy(out=xcb[:, hc[h]], in_=XC[:, hc[h]]), reads=[XC_b[h]], writes=[xcb_b[h]])
                            for tt in range(4 * h, 4 * h + 4):
                                ps = L_rot.get()
                                mm(ps[0][:], bda[:], xcb[:, tt * TT:(tt + 1) * TT], True, True, [bda_b, xcb_b[h]], [ps[1]])
                                sc.op(act, lambda: nc.scalar.activation(out=lxt[:, 3 + tt * TT:3 + (tt + 1) * TT], in_=ps[0][:], func=AF.Sigmoid,
                                                                        bias=prm[:, P_GA + pc:P_GA + pc + 1]),
                                      reads=[ps[1], prm_b], writes=[lxb[h]])
                            for tt in range(4 * h, 4 * h + 4):
                                ps = L_rot.get()
                                mm(ps[0][:], bdx[:], xcb[:, tt * TT:(tt + 1) * TT], True, True, [bdx_b, xcb_b[h]], [ps[1]])
                                sc.op(act, lambda: nc.scalar.activation(out=IG[:, tt * TT:(tt + 1) * TT], in_=ps[0][:], func=AF.Sigmoid,
                                                                        bias=prm[:, P_GX + pc:P_GX + pc + 1]),
                                      reads=[ps[1], prm_b], writes=[IG_b[h]])
                        nxt_tiles = lx_mm(c + 1) if c + 1 < NLC else None
                        for h in range(2):
                            r_ = zd[h]
                            p_ = PP[:, hc[h]]
                            sc.op(dve, lambda: nc.vector.tensor_scalar(out=p_, in0=r_, scalar1=ckt[:, 4, pc:pc + 1], scalar2=ckt[:, 3, pc:pc + 1],
                                                                       op0=ALU.mult, op1=ALU.add), reads=[lxb[h], ck_b], writes=[PP_b[h]])
                            sc.op(dve, lambda: nc.vector.tensor_tensor(out=p_, in0=p_, in1=r_, op=ALU.mult), reads=[PP_b[h], lxb[h]], writes=[PP_b[h]])
                            for k_ in (2, 1, 0):
                                sc.op(dve, lambda k_=k_: nc.vector.scalar_tensor_tensor(out=p_, in0=p_, scalar=ckt[:, k_, pc:pc + 1], in1=r_,
                                                                                        op0=ALU.add, op1=ALU.mult),
                                      reads=[PP_b[h], lxb[h], ck_b], writes=[PP_b[h]])
                            sc.op(act, lambda: nc.scalar.add(out=r_, in_=p_, add=1.0), reads=[PP_b[h]], writes=[lxb[h]])
                            sc.op(dve, lambda: nc.vector.scalar_tensor_tensor(out=p_, in0=p_, scalar=2.0, in1=p_, op0=ALU.add, op1=ALU.mult),
                                  reads=[PP_b[h]], writes=[PP_b[h]])
                            sc.op(act, lambda: nc.scalar.activation(out=p_, in_=p_, func=AF.Sqrt, scale=-1.0), reads=[PP_b[h]], writes=[PP_b[h]])
                            if nxt_tiles is not None:
                                lx_evac(c + 1, nxt_tiles, range(4 * h, 4 * h + 4))
                        if c + 1 < NLC:
                            gl_proj(c + 1)
                        for h in range(2):
                            sc.op(dve, lambda h=h: nc.vector.tensor_tensor(out=IG[:, hc[h]], in0=IG[:, hc[h]], in1=XC[:, hc[h]], op=ALU.mult),
                                  reads=[IG_b[h], XC_b[h]], writes=[IG_b[h]])
                            sc.op(dve, lambda h=h: nc.vector.tensor_tensor(out=IG[:, hc[h]], in0=IG[:, hc[h]], in1=PP[:, hc[h]], op=ALU.mult),
                                  reads=[IG_b[h], PP_b[h]], writes=[IG_b[h]])
                        sc.op(dve, lambda: nc.vector.tensor_tensor_scan(out=XC[:], data0=lxt[:, 3:3 + S], data1=IG[:], initial=0.0,
                                                                        op0=ALU.mult, op1=ALU.add),
                              reads=[lxb[0], lxb[1], IG_b[0], IG_b[1], XC_b[0], XC_b[1]], writes=[XC_b[0], XC_b[1]])
                        yb, yb_b = ybf[c % 2], ybf_b[c % 2]
                        for h in range(2):
                            sc.op(dve, lambda h=h: nc.vector.tensor_tensor(out=yb[:, hc[h]], in0=XC[:, hc[h]], in1=GL[c % 2][:, hc[h]], op=ALU.mult),
                                  reads=[XC_b[h], GL_b[c % 2]], writes=[yb_b])
                        sc.dma("sp", mixs[2 + c], yb[:], reads=[yb_b], writes=[mixd_b[2 + c]])
                        if upto == "lru%d" % c:
                            sc.barrier()
                            return
                    sc.barrier()
                    if upto == "lru":
                        return

                with ExitStack() as s3:
                    acc = [sb(f"acc{i}", [128, S], F32, s3) for i in range(2)]
                    acc_b = [Buf(f"acc{i}") for i in range(2)]
                    qp = [sb(f"qp{s_}", [128, 32, 256], BF16, s3) for s_ in range(2)]
                    qp_b = [Buf(f"qp{s_}") for s_ in range(2)]
                    kTs = [sb(f"kT{s_}", [128, S], BF16, s3) for s_ in range(2)]
                    kT_bs = [Buf(f"kT{s_}") for s_ in range(2)]
                    va = [sb(f"va{s_}", [128, 32, 192], BF16, s3) for s_ in range(2)]
                    va_b = [Buf(f"va{s_}") for s_ in range(2)]
                    em_rot = Rot([(sb(f"em{i}", [128, 512], F32, s3), Buf(f"em{i}")) for i in range(2)])
                    ex_rot = Rot([(sb(f"ex{i}", [128, 512], F32, s3), Buf(f"ex{i}")) for i in range(2)])
                    pt_rot = Rot([(sb(f"pt{i}", [128, 512], BF16, s3), Buf(f"pt{i}")) for i in range(3)])
                    sq_rot = Rot([(sb(f"asq{i}", [128, TT], BF16, s3), Buf(f"asq{i}")) for i in range(2)])
                    sd_rot = Rot([(sb(f"asd{i}", [128, TT], F32, s3), Buf(f"asd{i}")) for i in range(2)])
                    tmp_rot = Rot([(sb(f"atm{i}", [128, TT], F32, s3), Buf(f"atm{i}")) for i in range(1)])
                    att_rot = Rot([(sb(f"att{i}", [128, TT], BF16, s3), Buf(f"att{i}")) for i in range(2)])
                    for s_ in range(2):
                        sc.op(dve, lambda s_=s_: nc.vector.memset(va[s_][:, :, 64:128], 1.0), writes=[va_b[s_]])
                        sc.op(dve, lambda s_=s_: nc.vector.memset(qp[s_][64:128, :, 0:128], 0.0), writes=[qp_b[s_]])
                        sc.op(dve, lambda s_=s_: nc.vector.memset(qp[s_][0:64, :, 128:256], 0.0), writes=[qp_b[s_]])
                    ST_rot = Rot([PS[4], PS[5]])
                    A_rot = Rot([PS[0], PS[1], PS[2]])
                    PSTAT = PS[3]
                    PO_rot = Rot([PS[6], PS[7]])
                    em_of = {}

                    def proj_gen(hp, g, s_):
                        win_, r = GROUPS[g]
                        nb = (S // r) // 128
                        kT, kT_b = kTs[s_], kT_bs[s_]
                        vA, vA_b = va[s_], va_b[s_]
                        emt, emt_b = em_rot.get()
                        sc.dma("sp", emt[:], emask_d[g * 2 + hp], writes=[emt_b])
                        em_of[(hp, g)] = (emt, emt_b)
                        wq_ = load_w(2 * g + hp)
                        wk_ = load_w(6 + 2 * g + hp)
                        items = [(isq, tt) for isq in range(2) for tt in range(NT)]

                        def stepA(isq, tt):
                            w, w_b = (wq_, wk_)[isq]
                            ps = A_rot.get()
                            proj_tile(w, w_b, tt, ps)
                            return ps

                        def stepB(isq, tt, ps):
                            ncol = (P_QN + l, P_KN + l)[isq]
                            sd, sd_b = sd_rot.get()
                            rms_stats([ps[0][:]], ps[1], sq_rot, PSTAT, sd, sd_b, 1.0 / 64.0, 1, 1, lnexp=True)
                            if isq == 1:
                                if r == 1:
                                    o_ap = kT[:, tt * TT:(tt + 1) * TT]
                                    i_ap = ps[0][:]
                                    d_ap = sd[:]
                                else:
                                    na = TT // r
                                    o_ap = kT[:].rearrange("p (c l) -> p c l", c=r)[:, :, tt * na:(tt + 1) * na]
                                    i_ap = ps[0][:].rearrange("p (a c) -> p c a", c=r)
                                    d_ap = sd[:].rearrange("p (a c) -> p c a", c=r)
                                sc.op(dve, lambda: nc.vector.scalar_tensor_tensor(
                                    out=o_ap, in0=i_ap, scalar=prm[:, ncol:ncol + 1], in1=d_ap, op0=ALU.mult, op1=ALU.mult),
                                    reads=[ps[1], sd_b, prm_b], writes=[kT_b])
                            else:
                                for hh in range(2):
                                    rows = slice(hh * 64, hh * 64 + 64)
                                    hoff = hh * 128
                                    if r == 1:
                                        o_ap = qp[s_][rows, 4 * tt:4 * tt + 4, hoff:hoff + 128]
                                        i_ap = ps[0][rows, :].rearrange("p (j i) -> p j i", j=4)
                                        d_ap = sd[rows, :].rearrange("p (j i) -> p j i", j=4)
                                    elif r == 4:
                                        o_ap = qp[s_][rows, :, :].rearrange("p (c t) e -> p c t e", c=4)[:, :, tt, hoff:hoff + 128]
                                        i_ap = ps[0][rows, :].rearrange("p (a c) -> p c a", c=4)
                                        d_ap = sd[rows, :].rearrange("p (a c) -> p c a", c=4)
                                    else:
                                        i0 = hoff + 32 * (tt % 4)
                                        o_ap = qp[s_][rows, :, :].rearrange("p (c t) e -> p c t e", c=16)[:, :, tt // 4, i0:i0 + 32]
                                        i_ap = ps[0][rows, :].rearrange("p (a c) -> p c a", c=16)
                                        d_ap = sd[rows, :].rearrange("p (a c) -> p c a", c=16)
                                    sc.op(dve, lambda: nc.vector.scalar_tensor_tensor(
                                        out=o_ap, in0=i_ap, scalar=prm[rows, ncol:ncol + 1], in1=d_ap, op0=ALU.mult, op1=ALU.mult),
                                        reads=[ps[1], sd_b, prm_b], writes=[qp_b[s_]])

                        nxt_ps = stepA(*items[0])
                        for idx, it_ in enumerate(items):
                            cur_ps = nxt_ps
                            if idx + 1 < len(items):
                                nxt_ps = stepA(*items[idx + 1])
                            stepB(it_[0], it_[1], cur_ps)
                            yield
                        wv, wv_b = load_w(12 + 2 * g + hp)
                        for b4 in range(8):
                            pv, pvb = A_rot.get()
                            for j in range(4):
                                blk = b4 * 4 + j
                                c_, kb = divmod(blk, nb)
                                t0 = c_ + r * 128 * kb
                                tts = list(range(t0 // TT, (t0 + r * 127) // TT + 1))
                                for k in range(KC):
                                    base = k * S + t0
                                    mm(pv[:, j * 128:(j + 1) * 128], hT_flat[:, base:base + 127 * r + 1:r], wv[:, k, :], k == 0, k == KC - 1,
                                       [hT_b[t_] for t_ in tts] + [wv_b], [pvb])
                            pv3 = pv[:].rearrange("p (j e) -> p j e", j=4)
                            sc.op(act, lambda: nc.scalar.copy(out=vA[:, b4 * 4:b4 * 4 + 4, 0:64], in_=pv3[:, :, 0:64]),
                                  reads=[pvb], writes=[vA_b])
                            sc.op(act, lambda: nc.scalar.copy(out=vA[:, b4 * 4:b4 * 4 + 4, 128:192], in_=pv3[:, :, 64:128]),
                                  reads=[pvb], writes=[vA_b])
                            yield

                    LOOK = 1
                    PBURST = 8

                    def attn_gen(hp, g, s_):
                        win_, r = GROUPS[g]
                        nb = (S // r) // 128
                        kT, kT_b = kTs[s_], kT_bs[s_]
                        vA, vA_b = va[s_], va_b[s_]
                        emt, emt_b = em_of[(hp, g)]

                        def qk_block(blk):
                            c_, qb = divmod(blk, nb)
                            cur = slice(blk * 128, (blk + 1) * 128)
                            prv = slice((blk - 1) * 128, blk * 128)
                            stt, stb = ST_rot.get()
                            mm(stt[:, 0:256], kT[:, cur], qp[s_][:, blk, :], True, True, [kT_b, qp_b[s_]], [stb])
                            if qb > 0:
                                mm(stt[:, 256:512], kT[:, prv], qp[s_][:, blk, :], True, True, [kT_b, qp_b[s_]], [stb])
                            return stt, stb

                        pend = [qk_block(b_) for b_ in range(LOOK)]
                        pending_acc = None
                        po, po_b = None, None
                        for blk in range(32):
                            stt, stb = pend.pop(0)
                            if blk + LOOK < 32:
                                pend.append(qk_block(blk + LOOK))
                            c_, qb = divmod(blk, nb)
                            j = blk % 2
                            if j == 0:
                                po, po_b = PO_rot.get()
                            ex, ex_b = ex_rot.get()
                            sc.op(act, lambda: nc.scalar.activation(out=ex[:], in_=stt[:], func=AF.Exp, scale=0.125),
                                  reads=[stb], writes=[ex_b])
                            pt, pt_b = pt_rot.get()
                            sc.op(dve, lambda: nc.vector.tensor_tensor(out=pt[:], in0=ex[:], in1=emt[:], op=ALU.mult),
                                  reads=[ex_b, emt_b], writes=[pt_b])
                            if pending_acc is not None:
                                pending_acc()
                                pending_acc = None
                            for hh in range(2):
                                vc = slice(hh * 64, hh * 64 + 128)
                                oc = slice(hh * 256 + j * 128, hh * 256 + (j + 1) * 128)
                                mm(po[:, oc], vA[:, blk, vc], pt[:, hh * 128:hh * 128 + 128], True, qb == 0, [vA_b, pt_b], [po_b])
                                if qb > 0:
                                    mm(po[:, oc], vA[:, blk - 1, vc], pt[:, 256 + hh * 128:256 + hh * 128 + 128], False, True, [vA_b, pt_b], [po_b])
                            if j == 1:
                                def _acc(blk=blk, po=po, po_b=po_b):
                                    blk0 = blk - 1
                                    c0, qb0 = divmod(blk0, nb)
                                    start = c0 + r * 128 * qb0
                                    for hh in range(2):
                                        a_ap = acc[hh][:, start:start + 255 * r + 1:r]
                                        p_ap = po[:, hh * 256:(hh + 1) * 256]
                                        if g == 0:
                                            sc.op(act, lambda: nc.scalar.copy(out=a_ap, in_=p_ap), reads=[po_b], writes=[acc_b[hh]])
                                        else:
                                            sc.op(dve, lambda: nc.vector.tensor_tensor(out=a_ap, in0=a_ap, in1=p_ap, op=ALU.add),
                                                  reads=[po_b, acc_b[hh]], writes=[acc_b[hh]])
                                pending_acc = _acc
                            yield
                        if pending_acc is not None:
                            pending_acc()

                    def normalize(hp):
                        for tt in range(NT):
                            cols = slice(tt * TT, (tt + 1) * TT)
                            tm, tm_b = tmp_rot.get()
                            at, at_b = att_rot.get()
                            sc.op(dve, lambda: nc.vector.tensor_copy(out=tm[0:64, :], in_=acc[0][64:128, cols]), reads=[acc_b[0]], writes=[tm_b])
                            sc.op(dve, lambda: nc.vector.tensor_copy(out=tm[64:128, :], in_=acc[1][0:64, cols]), reads=[acc_b[1]], writes=[tm_b])
                            sc.op(act, lambda: nc.scalar.activation(out=tm[:], in_=tm[:], func=AF.Ln), reads=[tm_b], writes=[tm_b])
                            sc.op(act, lambda: nc.scalar.activation(out=tm[:], in_=tm[:], func=AF.Exp, scale=-1.0), reads=[tm_b], writes=[tm_b])
                            sc.op(dve, lambda: nc.vector.tensor_tensor(out=at[0:64, :], in0=acc[0][0:64, cols], in1=tm[0:64, :], op=ALU.mult),
                                  reads=[acc_b[0], tm_b], writes=[at_b])
                            sc.op(dve, lambda: nc.vector.tensor_tensor(out=at[64:128, :], in0=acc[1][64:128, cols], in1=tm[64:128, :], op=ALU.mult),
                                  reads=[acc_b[1], tm_b], writes=[at_b])
                            sc.dma("sp", mixs[hp][:, cols], at[:], reads=[at_b], writes=[mixd_b[hp]])

                    seq = [(hp, g) for hp in range(2) for g in range(3)]
                    for _ in proj_gen(seq[0][0], seq[0][1], 0):
                        pass
                    for i_, (hp, g) in enumerate(seq):
                        A = attn_gen(hp, g, i_ % 2)
                        P = proj_gen(seq[i_ + 1][0], seq[i_ + 1][1], (i_ + 1) % 2) if i_ + 1 < len(seq) else None
                        n_p = 24
                        done_p = 0
                        for step, _ in enumerate(A):
                            if P is not None:
                                want = ((((step + 1) * n_p) // 32) // PBURST) * PBURST
                                while done_p < want:
                                    next(P, None)
                                    done_p += 1
                        if P is not None:
                            for _ in P:
                                pass
                        if g == 2:
                            normalize(hp)
                    sc.barrier()
                    if upto == "attn":
                        return
            if m4_hooks is not None:
                m4_hooks[0]()
            with ExitStack() as s4:
                wo = sb("wo", [128, KC, D], BF16, s4)
                wo_b = [Buf(f"wo{k}") for k in range(KC)]
                xts = [sb(f"ox{i}", [128, KC, TT], F32, s4) for i in range(2)]
                xt_b = [Buf(f"ox{i}") for i in range(2)]
                mts = [sb(f"om{i}", [128, KC, TT], BF16, s4) for i in range(2)]
                mt_b = [Buf(f"om{i}") for i in range(2)]
                Y_rot = Rot([PS[0], PS[1], PS[2], PS[3]])
                for k in range(KC):
                    sc.dma("pool", wo[:, k, :], wout_d[l, k], writes=[wo_b[k]])
                if m4_hooks is not None:
                    m4_hooks[1]()

                def load(tt):
                    i = tt % 2
                    sc.dma("sp", xts[i][:], y_view[:, :, tt * TT:(tt + 1) * TT], reads=[xd_b[tt]], writes=[xt_b[i]])
                    sc.dma("sp", mts[i][:], mix_view[:, :, tt * TT:(tt + 1) * TT], reads=mixd_b, writes=[mt_b[i]])

                load(0)
                for tt in range(NT):
                    i = tt % 2
                    if tt + 1 < NT:
                        load(tt + 1)
                    for d in range(KC):
                        py, pyb = Y_rot.get()
                        for k in range(KC):
                            mm(py[:], wo[:, k, d * 128:(d + 1) * 128], mts[i][:, k, :], k == 0, k == KC - 1, [wo_b[k], mt_b[i]], [pyb])
                        sc.op(dve, lambda d=d: nc.vector.tensor_tensor(out=xts[i][:, d, :], in0=py[:], in1=xts[i][:, d, :], op=ALU.add),
                              reads=[pyb, xt_b[i]], writes=[xt_b[i]])
                    sc.dma("sp", y_view[:, :, tt * TT:(tt + 1) * TT], xts[i][:], reads=[xt_b[i]], writes=[xd_b[tt]])
                sc.barrier()

        done = False
        for l in range(n_layers):
            ffn_phase(l, 0, x_view_in if l == 0 else y_view, first=(l == 0))
            if stop_after == (l, "ffn1"):
                done = True
                break
            if stop_after is not None and stop_after[0] == l and len(stop_after) == 3:
                mixer_phase(l, stop_after[2])
                break
            if stop_after == (l, "mixer"):
                mixer_phase(l)
                done = True
                break
            with ExitStack() as wst:
                hold = {}

                def _alloc():
                    hold["W"] = ffn_alloc(wst)

                def _load(l=l):
                    ffn_load(hold["W"], l, 1)

                if PREFETCH_FFN2:
                    mixer_phase(l, m4_hooks=(_alloc, _load))
                    ffn_phase(l, 1, y_view, first=False, W=hold["W"])
                else:
                    mixer_phase(l)
                    ffn_phase(l, 1, y_view, first=False)
        sc.barrier()
        build_program.stats = dict(cnt={n: e.cnt for n, e in sc.engs.items()}, waits=sc.n_wait)
    return nc


def _slopes():
    h = np.arange(1, 13, dtype=np.float64)
    return np.exp2(-8.0 * h / 12.0).reshape(3, 4)


def _emask():
    sl = _slopes()
    i = np.arange(128)[:, None].astype(np.float64)
    j = np.arange(128)[None, :].astype(np.float64)
    out = np.zeros((6, 128, 512), np.float32)
    for g, (_, r) in enumerate(GROUPS):
        for hp in range(2):
            for hh in range(2):
                a = sl[g, 2 * hp + hh] * r
                cur = np.where(j >= i, np.exp(-a * (j - i)), 0.0)
                prv = np.where(i >= j, np.exp(-a * (128.0 + j - i)), 0.0)
                out[g * 2 + hp, :, hh * 128:hh * 128 + 128] = cur
                out[g * 2 + hp, :, 256 + hh * 128:256 + hh * 128 + 128] = prv
    return out


def _prep_weights(inp):
    f32 = np.float32
    c = np.ascontiguousarray
    out = {}
    for i, nm in ((1, "ffn1"), (2, "ffn2")):
        for key, src in ((f"wg{i}", f"{nm}_w_gate"), (f"wu{i}", f"{nm}_w_up")):
            w = np.asarray(inp[src], f32).reshape(DEPTH, KC, 128, NF, 128)
            out[key] = c(w.transpose(0, 3, 2, 1, 4))
        out[f"wd{i}"] = c(np.asarray(inp[f"{nm}_w_down"], f32).reshape(DEPTH, NF, 128, D))
    w = np.asarray(inp["w_in"], f32).reshape(DEPTH, KC, 128, 30, 128)
    out["win"] = c(w.transpose(0, 3, 2, 1, 4))
    out["wout"] = c(np.asarray(inp["w_out"], f32).reshape(DEPTH, KC, 128, D))
    for key, src in (("bda", "gate_a_w"), ("bdx", "gate_x_w")):
        gw = np.asarray(inp[src], f32)
        bd = np.zeros((DEPTH, NLC, 128, 128), f32)
        for cidx in range(NLC):
            bd[:, cidx, 0:64, 0:64] = gw[:, 2 * cidx]
            bd[:, cidx, 64:128, 64:128] = gw[:, 2 * cidx + 1]
        out[key] = bd
    prm = np.zeros((128, NPRM), f32)
    for base, src in ((P_G1, "ffn1_norm"), (P_GM, "mix_norm"), (P_G2, "ffn2_norm")):
        g_ = np.asarray(inp[src], f32).reshape(DEPTH, KC, 128)
        prm[:, base:base + DEPTH * KC] = g_.transpose(2, 0, 1).reshape(128, DEPTH * KC)
    prm[:, P_QN:P_QN + DEPTH] = np.tile(np.asarray(inp["q_norm"], f32).T, (2, 1))
    prm[:, P_KN:P_KN + DEPTH] = np.tile(np.asarray(inp["k_norm"], f32).T, (2, 1))
    cw = np.asarray(inp["conv_w"], f32).reshape(DEPTH, 4, NLC, 128)
    prm[:, P_CW:P_CW + DEPTH * NLC * 4] = cw.transpose(3, 0, 2, 1).reshape(128, DEPTH * NLC * 4)
    for base, src in ((P_CB, "conv_b"), (P_GA, "gate_a_b"), (P_GX, "gate_x_b"), (P_LM, "lru_lambda")):
        v = np.asarray(inp[src], f32).reshape(DEPTH, NLC, 128)
        prm[:, base:base + DEPTH * NLC] = v.transpose(2, 0, 1).reshape(128, DEPTH * NLC)
    out["prm"] = prm
    out["emask"] = _emask()
    ob = np.zeros((2, 128, 128), f32)
    ob[0] = 1.0
    ob[1, 0:64, 0:64] = 1.0
    ob[1, 64:128, 64:128] = 1.0
    out["onesbd"] = ob
    return out


def _run(inp, cores, n_layers=DEPTH, stop_after=None, trace=False):
    shared = _prep_weights(inp)
    x = np.asarray(inp["x"], np.float32)
    nc = build_program(n_layers, stop_after)
    in_maps = []
    for b in cores:
        m = dict(shared)
        m["xT"] = np.ascontiguousarray(x[b].T).reshape(KC, 128, S)
        in_maps.append(m)
    res = run_bass_kernel_spmd(nc, in_maps, core_ids=list(range(len(cores))), trace=trace)
    outs = [np.ascontiguousarray(r["y"].reshape(D, S).T) for r in res.results]
    return outs, res


def kernel(**inputs):
    outs, _ = _run(inputs, list(range(NCORES)))
    return np.stack(outs, axis=0).astype(np.float32)
```
